# Optimizing a Trainium2 kernel written in Bass

```python
import math
import jax, jax.numpy as jnp
from jax import lax
import numpy as np

D_MODEL = 2048
BATCH = 4
SEQ = 4096
DEPTH = 4

GRID_W = 64
CTX_LEN = 256
N_MIXERS = 3
N_GA = len(range(0, DEPTH, N_MIXERS))
N_RT = len(range(1, DEPTH, N_MIXERS))
N_DF = len(range(2, DEPTH, N_MIXERS))

EPS = 1e-6
NEG_INF = -1e30
ROPE_THETA = 10000.0
BLOCK = 128

GA_HD = 128
GA_HEADS = D_MODEL // GA_HD
GA_KV = GA_HEADS // 4
GA_GROUP = GA_HEADS // GA_KV
WINDOW = 128

RT_HEADS = 8
RT_QK = D_MODEL // RT_HEADS
RT_V = 2 * RT_QK
CHUNK = 128

DF_HD = 128
DF_HEADS = D_MODEL // (2 * DF_HD)

D_FF = 256 * math.ceil(8 * D_MODEL / (3 * 256))
CONV_W = 3

kernel_name = "hybrid_interleaved_dit_prefix_block"


def rms_norm(x, g):
    xf = x.astype(jnp.float32)
    y = xf * lax.rsqrt(jnp.mean(xf * xf, -1, keepdims=True) + EPS)
    return (y * g.astype(jnp.float32)).astype(x.dtype)


def head_group_norm(o, g):
    mu = jnp.mean(o, -1, keepdims=True)
    d = o - mu
    var = jnp.mean(d * d, -1, keepdims=True)
    return d * lax.rsqrt(var + EPS) * g.astype(jnp.float32).reshape(o.shape[2:])


def modulate(h, shift, scale):
    return h * (1 + scale) + shift


def adaln(cond, w, b):
    m = jax.nn.silu(cond) @ w + b
    return jnp.split(m, 6, -1)


def rope(x, cos, sin):
    half = x.shape[-1] // 2
    xf = x.astype(jnp.float32)
    x1, x2 = xf[..., :half], xf[..., half:]
    return jnp.concatenate([x1 * cos - x2 * sin, x1 * sin + x2 * cos], -1).astype(x.dtype)


def axial_rope_tables(rows_count, head_dim):
    rows = jnp.repeat(jnp.arange(rows_count), GRID_W).astype(jnp.float32)
    cols = jnp.tile(jnp.arange(GRID_W), rows_count).astype(jnp.float32)
    n_freq = head_dim // 4
    inv = ROPE_THETA ** (-jnp.arange(n_freq, dtype=jnp.float32) / n_freq)
    ang = jnp.concatenate([rows[:, None] * inv, cols[:, None] * inv], -1)
    return jnp.cos(ang), jnp.sin(ang)


def linear_rope_tables(n_tokens, head_dim):
    half = head_dim // 2
    inv = ROPE_THETA ** (-jnp.arange(half, dtype=jnp.float32) / half)
    ang = jnp.arange(n_tokens, dtype=jnp.float32)[:, None] * inv
    return jnp.cos(ang), jnp.sin(ang)


def joint_softmax(parts, sink=None):
    m = parts[0].max(-1, keepdims=True)
    for p in parts[1:]:
        m = jnp.maximum(m, p.max(-1, keepdims=True))
    if sink is not None:
        m = jnp.maximum(m, sink)
    es = [jnp.exp(p - m) for p in parts]
    den = sum(e.sum(-1, keepdims=True) for e in es)
    if sink is not None:
        den = den + jnp.exp(sink - m)
    return [e / den for e in es]


def windowed_gqa_mixer(h_lat, h_ctx, wqkv, sink, qk_g, wo, cos, sin, need_ctx):
    B, S, _ = h_lat.shape
    G = GA_GROUP
    scale = GA_HD ** -0.5

    def project(h):
        n = h.shape[1]
        q, k, v = jnp.split(h @ wqkv, [GA_HEADS * GA_HD, (GA_HEADS + GA_KV) * GA_HD], -1)
        q = rms_norm(q.reshape(B, n, GA_KV, G, GA_HD), qk_g[0]) * scale
        k = rms_norm(k.reshape(B, n, GA_KV, GA_HD), qk_g[1])
        return q, k, v.reshape(B, n, GA_KV, GA_HD)

    qc, kc, vc = project(h_ctx)
    ql, kl, vl = project(h_lat)
    ql = rope(ql, cos[:, None, None], sin[:, None, None])
    kl = rope(kl, cos[:, None], sin[:, None])
    sink_b = sink.astype(jnp.float32).reshape(GA_KV, G)[:, :, None, None]

    nb = S // BLOCK
    qb = ql.reshape(B, nb, BLOCK, GA_KV, G, GA_HD)

    def band(t):
        tp = jnp.pad(t, ((0, 0), (BLOCK, BLOCK), (0, 0), (0, 0))).reshape(B, nb + 2, BLOCK, *t.shape[2:])
        return jnp.concatenate([tp[:, :-2], tp[:, 1:-1], tp[:, 2:]], axis=2)

    kw, vw = band(kl), band(vl)
    qpos = jnp.arange(S).reshape(nb, BLOCK)[:, :, None]
    kpos = (jnp.arange(nb)[:, None] * BLOCK - BLOCK + jnp.arange(3 * BLOCK)[None, :])[:, None, :]
    mask = (jnp.abs(kpos - qpos) <= WINDOW) & (kpos >= 0) & (kpos < S)
    s_loc = jnp.einsum("bcingd,bcjnd->bcngij", qb, kw).astype(jnp.float32)
    s_loc = jnp.where(mask[None, :, None, None], s_loc, NEG_INF)
    s_ctx = jnp.einsum("bcingd,bjnd->bcngij", qb, kc).astype(jnp.float32)
    p_loc, p_ctx = joint_softmax([s_loc, s_ctx], sink_b)
    o = (jnp.einsum("bcngij,bcjnd->bcingd", p_loc.astype(vw.dtype), vw)
         + jnp.einsum("bcngij,bjnd->bcingd", p_ctx.astype(vc.dtype), vc))
    y_lat = o.reshape(B, S, GA_HEADS * GA_HD) @ wo

    y_ctx = None
    if need_ctx:
        s = jnp.einsum("bingd,bjnd->bngij", qc, kc).astype(jnp.float32)
        (p,) = joint_softmax([s], sink_b)
        oc = jnp.einsum("bngij,bjnd->bingd", p.astype(vc.dtype), vc)
        y_ctx = oc.reshape(B, h_ctx.shape[1], GA_HEADS * GA_HD) @ wo
    return y_lat, y_ctx


def retention_scan(q, k, v, log_gamma, state0):
    B, N, H, dk = q.shape
    dv = v.shape[-1]
    nc = N // CHUNK
    pos = jnp.arange(CHUNK, dtype=jnp.float32)
    rel = pos[:, None] - pos[None, :]
    lg = log_gamma.astype(jnp.float32)
    decay_mask = jnp.where(rel >= 0, jnp.exp(lg[:, None, None] * jnp.maximum(rel, 0.0)), 0.0)
    q_decay = jnp.exp(lg[:, None] * (pos + 1))[:, :, None]
    k_decay = jnp.exp(lg[:, None] * (CHUNK - 1 - pos))[:, :, None]
    chunk_decay = jnp.exp(lg * CHUNK)[:, None, None]

    def to_chunks(t):
        return t.reshape(B, nc, CHUNK, H, t.shape[-1]).transpose(1, 0, 3, 2, 4)

    def step(state, inp):
        qi, ki, vi = inp
        inner = jnp.einsum("bhid,bhjd->bhij", qi, ki) * decay_mask
        o = (jnp.einsum("bhij,bhje->bhie", inner, vi)
             + jnp.einsum("bhid,bhde->bhie", qi, state) * q_decay)
        state = state * chunk_decay + jnp.einsum("bhjd,bhje->bhde", ki * k_decay, vi)
        return state, o

    state, o = lax.scan(step, state0, (to_chunks(q), to_chunks(k), to_chunks(v)))
    return o.transpose(1, 0, 3, 2, 4).reshape(B, N, H, dv), state


def retention_mixer(h_lat, h_ctx, w_in, decay_logit, gn_g, wo, cos, sin, need_ctx):
    B, S, _ = h_lat.shape
    QK, V = RT_HEADS * RT_QK, RT_HEADS * RT_V

    def project(h):
        n = h.shape[1]
        q, k, v, gf, gb = jnp.split(h @ w_in, [QK, 2 * QK, 2 * QK + V, 2 * QK + 2 * V], -1)
        q = q.reshape(B, n, RT_HEADS, RT_QK) * (RT_QK ** -0.5)
        return q, k.reshape(B, n, RT_HEADS, RT_QK), v.reshape(B, n, RT_HEADS, RT_V), gf, gb

    log_gamma = jax.nn.log_sigmoid(decay_logit.astype(jnp.float32))
    qc, kc, vc, gfc, gbc = project(h_ctx)
    ql, kl, vl, gfl, gbl = project(h_lat)
    ql = rope(ql, cos[:, None], sin[:, None])
    kl = rope(kl, cos[:, None], sin[:, None])
    flip = lambda t: jnp.flip(t, 1)
    zero = jnp.zeros((B, RT_HEADS, RT_QK, RT_V), jnp.float32)

    oc_f, st_f = retention_scan(qc, kc, vc, log_gamma[0], zero)
    oc_b, st_b = retention_scan(flip(qc), flip(kc), flip(vc), log_gamma[1], zero)
    ol_f, _ = retention_scan(ql, kl, vl, log_gamma[0], st_f)
    ol_b, _ = retention_scan(flip(ql), flip(kl), flip(vl), log_gamma[1], st_b)

    def combine(of, ob, gf, gb):
        n = of.shape[1]
        yf = head_group_norm(of, gn_g[0]).reshape(B, n, V).astype(gf.dtype)
        yb = head_group_norm(ob, gn_g[1]).reshape(B, n, V).astype(gb.dtype)
        return (jax.nn.silu(gf) * yf + jax.nn.silu(gb) * yb) @ wo

    y_lat = combine(ol_f, flip(ol_b), gfl, gbl)
    y_ctx = combine(oc_f, flip(oc_b), gfc, gbc) if need_ctx else None
    return y_lat, y_ctx


def diff_attention_mixer(h_lat, h_ctx, wqkv, lam, qk_g, subln_g, wo, cos, sin, lambda_init, need_ctx):
    B, S, _ = h_lat.shape
    scale = DF_HD ** -0.5

    def project(h):
        n = h.shape[1]
        q, k, v = jnp.split(h @ wqkv, [2 * DF_HEADS * DF_HD, 4 * DF_HEADS * DF_HD], -1)
        q = rms_norm(q.reshape(B, n, DF_HEADS, 2, DF_HD), qk_g[0]) * scale
        k = rms_norm(k.reshape(B, n, DF_HEADS, 2, DF_HD), qk_g[1])
        return q, k, v.reshape(B, n, DF_HEADS, 2 * DF_HD)

    lam_f = lam.astype(jnp.float32)
    lmbda = jnp.exp(jnp.sum(lam_f[0] * lam_f[1])) - jnp.exp(jnp.sum(lam_f[2] * lam_f[3])) + lambda_init

    def attend(q, k_parts, v_parts):
        s = [jnp.einsum("bihrd,bjhrd->bhrij", q, kp).astype(jnp.float32) for kp in k_parts]
        probs = joint_softmax(s)
        return sum(jnp.einsum("bhij,bjhe->bihe", (p[:, :, 0] - lmbda * p[:, :, 1]).astype(vp.dtype), vp)
                   for p, vp in zip(probs, v_parts))

    def finish(o):
        o = rms_norm(o, subln_g) * (1.0 - lambda_init)
        return o.reshape(B, o.shape[1], DF_HEADS * 2 * DF_HD) @ wo

    qc, kc, vc = project(h_ctx)
    ql, kl, vl = project(h_lat)
    ql = rope(ql, cos[:, None, None], sin[:, None, None])
    kl = rope(kl, cos[:, None, None], sin[:, None, None])

    nb = S // BLOCK
    qb = ql.reshape(B, nb, BLOCK, DF_HEADS, 2, DF_HD).transpose(1, 0, 2, 3, 4, 5)
    o = lax.map(lambda qblk: attend(qblk, [kl, kc], [vl, vc]), qb)
    y_lat = finish(o.transpose(1, 0, 2, 3, 4).reshape(B, S, DF_HEADS, 2 * DF_HD))
    y_ctx = finish(attend(qc, [kc], [vc])) if need_ctx else None
    return y_lat, y_ctx


def conv_ffn(h, w_in, conv_w, conv_b, w_out):
    u = h @ w_in
    n = u.shape[1]
    pad = CONV_W // 2
    up = jnp.pad(u, ((0, 0), (pad, pad), (0, 0)))
    u = conv_b + sum(up[:, t:t + n] * conv_w[t] for t in range(CONV_W))
    a, b = jnp.split(u, 2, -1)
    return (jax.nn.silu(a) * b) @ w_out


def setup_inputs(seed: int = 0) -> dict:
    key = jax.random.key(seed)
    ks = iter(jax.random.split(key, 32))
    f32 = jnp.float32
    D = D_MODEL

    def nrm(shape, s):
        return jax.random.normal(next(ks), shape, f32) * s

    gamma0 = 1.0 - 2.0 ** (-5.0 - np.arange(RT_HEADS))
    decay_init = jnp.asarray(np.log(gamma0 / (1.0 - gamma0)), f32)
    ga_q = GA_HEADS * GA_HD
    return {
        "x": nrm((BATCH, SEQ, D), 1.0),
        "c": nrm((BATCH, D), 1.0),
        "ctx": nrm((BATCH, CTX_LEN, D), 1.0),
        "c_ctx": nrm((D,), 1.0),
        "mod_w": nrm((DEPTH, D, 6 * D), 0.5 * D ** -0.5),
        "mod_b": nrm((DEPTH, 6 * D), 0.02),
        "norm_g": 1.0 + nrm((DEPTH, 2, D), 0.02),
        "ffn_w_in": nrm((DEPTH, D, 2 * D_FF), D ** -0.5),
        "ffn_conv_w": nrm((DEPTH, CONV_W, 2 * D_FF), CONV_W ** -0.5),
        "ffn_conv_b": nrm((DEPTH, 2 * D_FF), 0.02),
        "ffn_w_out": nrm((DEPTH, D_FF, D), D_FF ** -0.5),
        "ga_wqkv": nrm((N_GA, D, ga_q + 2 * GA_KV * GA_HD), D ** -0.5),
        "ga_sink": nrm((N_GA, GA_HEADS), 0.5),
        "ga_qk_norm": 1.0 + nrm((N_GA, 2, GA_HD), 0.02),
        "ga_wo": nrm((N_GA, ga_q, D), ga_q ** -0.5),
        "rt_w_in": nrm((N_RT, D, 2 * RT_HEADS * RT_QK + 3 * RT_HEADS * RT_V), D ** -0.5),
        "rt_decay": decay_init[None, None, :] + nrm((N_RT, 2, RT_HEADS), 0.1),
        "rt_gn": 1.0 + nrm((N_RT, 2, RT_HEADS * RT_V), 0.02),
        "rt_wo": nrm((N_RT, RT_HEADS * RT_V, D), (RT_HEADS * RT_V) ** -0.5),
        "df_wqkv": nrm((N_DF, D, 6 * DF_HEADS * DF_HD), D ** -0.5),
        "df_lambda": nrm((N_DF, 4, DF_HD), 0.1),
        "df_qk_norm": 1.0 + nrm((N_DF, 2, DF_HD), 0.02),
        "df_subln": 1.0 + nrm((N_DF, 2 * DF_HD), 0.02),
        "df_wo": nrm((N_DF, 2 * DF_HEADS * DF_HD, D), (2 * DF_HEADS * DF_HD) ** -0.5),
    }


def reference(x, c, ctx, c_ctx, mod_w, mod_b, norm_g, ffn_w_in, ffn_conv_w, ffn_conv_b, ffn_w_out,
              ga_wqkv, ga_sink, ga_qk_norm, ga_wo, rt_w_in, rt_decay, rt_gn, rt_wo,
              df_wqkv, df_lambda, df_qk_norm, df_subln, df_wo):
    n_lat = x.shape[1]
    ROWS = n_lat // GRID_W
    ga_cos, ga_sin = axial_rope_tables(ROWS, GA_HD)
    df_cos, df_sin = axial_rope_tables(ROWS, DF_HD)
    rt_cos, rt_sin = linear_rope_tables(n_lat, RT_QK)

    h_lat, h_ctx = x, ctx
    for i in range(DEPTH):
        need_ctx = i < DEPTH - 1
        kind, j = i % N_MIXERS, i // N_MIXERS
        m_lat = [m[:, None, :] for m in adaln(c, mod_w[i], mod_b[i])]
        m_ctx = adaln(c_ctx, mod_w[i], mod_b[i])
        a_lat = modulate(rms_norm(h_lat, norm_g[i, 0]), m_lat[0], m_lat[1])
        a_ctx = modulate(rms_norm(h_ctx, norm_g[i, 0]), m_ctx[0], m_ctx[1])
        if kind == 0:
            y_lat, y_ctx = windowed_gqa_mixer(a_lat, a_ctx, ga_wqkv[j], ga_sink[j], ga_qk_norm[j], ga_wo[j],
                                              ga_cos, ga_sin, need_ctx)
        elif kind == 1:
            y_lat, y_ctx = retention_mixer(a_lat, a_ctx, rt_w_in[j], rt_decay[j], rt_gn[j], rt_wo[j],
                                           rt_cos, rt_sin, need_ctx)
        else:
            y_lat, y_ctx = diff_attention_mixer(a_lat, a_ctx, df_wqkv[j], df_lambda[j], df_qk_norm[j],
                                                df_subln[j], df_wo[j], df_cos, df_sin,
                                                0.8 - 0.6 * math.exp(-0.3 * i), need_ctx)
        h_lat = h_lat + m_lat[2] * y_lat
        f_lat = modulate(rms_norm(h_lat, norm_g[i, 1]), m_lat[3], m_lat[4])
        h_lat = h_lat + m_lat[5] * conv_ffn(f_lat, ffn_w_in[i], ffn_conv_w[i], ffn_conv_b[i], ffn_w_out[i])
        if need_ctx:
            h_ctx = h_ctx + m_ctx[2] * y_ctx
            f_ctx = modulate(rms_norm(h_ctx, norm_g[i, 1]), m_ctx[3], m_ctx[4])
            h_ctx = h_ctx + m_ctx[5] * conv_ffn(f_ctx, ffn_w_in[i], ffn_conv_w[i], ffn_conv_b[i], ffn_w_out[i])
    return h_lat
```

```python
import math
import numpy as np
import concourse.bass as bass
import concourse.mybir as mybir
from concourse.bass_utils import run_bass_kernel_spmd

F32 = mybir.dt.float32
BF16 = mybir.dt.bfloat16
AF = mybir.ActivationFunctionType
ALU = mybir.AluOpType

D = 2048
KD = 16
NCTX = 256
NLAT = 4096
T = NCTX + NLAT
DFF = 5632
EPS = 1e-6
TILES = [(0, 256)] + [(256 + 512 * i, 512) for i in range(8)]
ENGS = ["pe", "act", "dve", "pool", "sp"]
NL = 4
NCORES = 4
DEBUG_OUT = []
STOP_AFTER_MIXER = None
RT_AS_DF = False


class Res:
    __slots__ = ("w", "r")

    def __init__(self):
        self.w = None
        self.r = []


class Prog:
    def __init__(self, nc, n_dma_sems=10):
        self.nc = nc
        self.ops = {e: [] for e in ENGS}
        self.known = {e: {} for e in ENGS}
        self.n_dma_sems = n_dma_sems
        self.dma_rr = {"sp": 0, "pool": 0}
        self.semkeys = list(ENGS)
        for q in ("sp", "pool"):
            for i in range(n_dma_sems):
                self.semkeys.append(f"d_{q}_{i}")
        self.cnt = {k: 0 for k in self.semkeys}

    def _deps(self, reads, writes):
        evs = []
        for r in reads:
            if r.w is not None:
                evs.append(r.w)
        for w in writes:
            if w.w is not None:
                evs.append(w.w)
            evs.extend(w.r)
        return evs

    def _commit(self, ev, reads, writes):
        for r in reads:
            r.r.append(ev)
        for w in writes:
            w.w = ev
            w.r = []

    def _waits(self, eng, evs):
        kn = self.known[eng]
        need = {}
        for (k, v) in evs:
            if k == eng and eng == "pe":
                continue
            if kn.get(k, 0) < v and need.get(k, 0) < v:
                need[k] = v
        for k, v in need.items():
            kn[k] = v
        return list(need.items())

    def op(self, eng, fn, reads=(), writes=()):
        evs = self._deps(reads, writes)
        waits = self._waits(eng, evs)
        self.cnt[eng] += 1
        ev = (eng, self.cnt[eng])
        self.ops[eng].append((waits, fn, (eng, 1)))
        self._commit(ev, reads, writes)
        return ev

    def dma(self, q, fn, reads=(), writes=()):
        evs = self._deps(reads, writes)
        i = self.dma_rr[q]
        self.dma_rr[q] = (i + 1) % self.n_dma_sems
        k = f"d_{q}_{i}"
        if self.cnt[k] > 0:
            evs.append((k, self.cnt[k]))
        waits = self._waits(q, evs)
        self.cnt[k] += 16
        ev = (k, self.cnt[k])
        self.ops[q].append((waits, fn, (k, 16)))
        self._commit(ev, reads, writes)
        return ev

    def barrier(self):
        evs = [(k, v) for k, v in self.cnt.items() if v > 0]
        for e in ENGS:
            waits = self._waits(e, evs)
            if waits:
                self.ops[e].append((waits, None, None))

    def emit(self):
        from contextlib import ExitStack
        nc = self.nc
        sems = {}
        with ExitStack() as es:
            for k in self.semkeys:
                sems[k] = es.enter_context(nc.semaphore("s_" + k))
            block = es.enter_context(nc.Block())
            final = [(k, v) for k, v in self.cnt.items() if v > 0]

            def run(engname, e):
                for waits, fn, inc in self.ops[engname]:
                    for (k, v) in waits:
                        e.wait_ge(sems[k], v)
                    if fn is not None:
                        inst = fn(e)
                        inst.then_inc(sems[inc[0]], inc[1])

            @block.tensor
            def _(e):
                run("pe", e)

            @block.scalar
            def _(e):
                run("act", e)

            @block.vector
            def _(e):
                run("dve", e)

            @block.gpsimd
            def _(e):
                run("pool", e)

            @block.sync
            def _(e):
                run("sp", e)
                for (k, v) in final:
                    e.wait_ge(sems[k], v)


class Arena:
    def __init__(self, nc, nbytes):
        self.ap = nc.alloc_sbuf_tensor("arena", [128, nbytes // 2], BF16).ap()
        self.cap = nbytes // 2
        self.off = 0

    def alloc(self, shape, dt):
        n = 1
        for s in shape:
            n *= s
        n16 = n * (2 if dt == F32 else 1)
        n16 = (n16 + 15) // 16 * 16
        assert self.off + n16 <= self.cap, f"SBUF arena overflow {(self.off + n16) * 2}"
        v = self.ap[:, self.off:self.off + (n * (2 if dt == F32 else 1))]
        self.off += n16
        if dt == F32:
            v = v.bitcast(F32)
        if len(shape) == 2:
            v = v.rearrange("p (a b) -> p a b", a=shape[0])
        elif len(shape) == 3:
            v = v.rearrange("p (a b c) -> p a b c", a=shape[0], b=shape[1])
        elif len(shape) == 4:
            v = v.rearrange("p (a b c d) -> p a b c d", a=shape[0], b=shape[1], c=shape[2])
        return v


class Ctx:
    pass


def kview(ap2d):
    return ap2d.rearrange("(k p) n -> p k n", p=128)


def gemm(C, aT, KC, groups, passes, main_banks, slot_cols):
    P, A = C.P, C.A
    gmax = max(sum(n for _, n in g) for g in groups)
    asb = A.alloc([KC, gmax], BF16)
    slots = [A.alloc([KC, slot_cols], BF16) for _ in range(2)]
    r_slot = [Res(), Res()]
    aTv = kview(aT)
    wi = 0
    bank_pos = [0]
    deferred = [None]

    def next_banks(nb):
        b = [main_banks[(bank_pos[0] + i) % len(main_banks)] for i in range(nb)]
        bank_pos[0] = (bank_pos[0] + nb) % len(main_banks)
        return b

    def run_deferred():
        d = deferred[0]
        deferred[0] = None
        if d is not None:
            d()

    for g in groups:
        r_a = []
        loc = []
        o = 0
        for (off, n) in g:
            r = Res()
            P.dma("sp", lambda e, o=o, off=off, n=n: e.dma_start(out=asb[:, :, o:o + n], in_=aTv[:, :, off:off + n]),
                  writes=[r])
            r_a.append(r)
            loc.append(o)
            o += n
        work = []
        for ps_ in passes:
            if ps_["kind"] == "B":
                for si, (pieces, units) in enumerate(ps_["supers"]):
                    def load(s, pieces=pieces):
                        c = 0
                        for (W2d, c0, w) in pieces:
                            P.dma("pool", lambda e, s=s, c=c, W2d=W2d, c0=c0, w=w: e.dma_start(
                                out=slots[s][:, :, c:c + w], in_=kview(W2d)[:, :, c0:c0 + w]), writes=[r_slot[s]])
                            c += w

                    def comp(s, si=si, units=units, epi=ps_["epi"]):
                        for ui, unit in enumerate(units):
                            for ti, (off, n) in enumerate(g):
                                banks = next_banks(len(unit))

                                def mm(e, s=s, unit=unit, banks=banks, lo=loc[ti], n=n):
                                    for ci, co in enumerate(unit):
                                        for k in range(KC):
                                            i = e.matmul(C.ps[:, banks[ci], 0:n], slots[s][:, k, co:co + 128],
                                                         asb[:, k, lo:lo + n], start=(k == 0), stop=(k == KC - 1))
                                    return i
                                P.op("pe", mm, reads=[r_slot[s], r_a[ti]], writes=[C.rps[b] for b in banks])
                                run_deferred()
                                deferred[0] = epi(si, ui, (off, n), banks, ti, len(g))
                    work.append((load, comp, False))
                work[-1] = (work[-1][0], work[-1][1], True)
            else:
                for fi, (W2d, c0) in enumerate(ps_["cols"]):
                    def load(s, W2d=W2d, c0=c0):
                        P.dma("pool", lambda e, s=s, W2d=W2d, c0=c0: e.dma_start(
                            out=slots[s][:, :, 0:512], in_=kview(W2d)[:, :, c0:c0 + 512]), writes=[r_slot[s]])

                    def comp(s, fi=fi, epi=ps_["epi"]):
                        for ti, (off, n) in enumerate(g):
                            for tc in range(n // 128):
                                banks = next_banks(1)

                                def mm(e, s=s, b=banks[0], lo=loc[ti] + tc * 128):
                                    for k in range(KC):
                                        i = e.matmul(C.ps[:, b, :], asb[:, k, lo:lo + 128], slots[s][:, k, 0:512],
                                                     start=(k == 0), stop=(k == KC - 1))
                                    return i
                                P.op("pe", mm, reads=[r_slot[s], r_a[ti]], writes=[C.rps[banks[0]]])
                                run_deferred()
                                deferred[0] = epi(fi, off + tc * 128, banks[0])
                    work.append((load, comp, False))
                work[-1] = (work[-1][0], work[-1][1], True)
        slot_of = []
        for i in range(len(work)):
            slot_of.append(wi % 2)
            wi += 1
        if work:
            work[0][0](slot_of[0])
        for i, (load, comp, last_of_pass) in enumerate(work):
            if i + 1 < len(work):
                work[i + 1][0](slot_of[i + 1])
            comp(slot_of[i])
            if last_of_pass:
                run_deferred()


def phase_begin(C):
    C.P.barrier()
    C.A.off = C.persist_off
    for r in C.rps:
        r.w = None
        r.r = []


def make_resid_epi(C, gate_of, chunk_of):
    P, A = C.P, C.A
    NB = 3
    hts = [A.alloc([512], F32) for _ in range(NB)]
    hns = [A.alloc([512], F32) for _ in range(NB)]
    r_ht = [Res() for _ in range(NB)]
    r_hn = [Res() for _ in range(NB)]
    cnt = [0]

    def epi(si, ui, tile, banks, ti, nt):
        off, n = tile
        c = chunk_of(si, ui)
        i = cnt[0] % NB
        cnt[0] += 1
        col = 1 if off < NCTX else 0
        src = C.hT[c * 128:(c + 1) * 128, off:off + n]
        P.dma("sp", lambda e: e.dma_start(out=hts[i][:, 0:n], in_=src), writes=[r_ht[i]])
        g = gate_of(c, col)
        b = banks[0]
        P.op("dve", lambda e: e.scalar_tensor_tensor(out=hns[i][:, 0:n], in0=C.ps[:, b, 0:n], scalar=g,
                                                     in1=hts[i][:, 0:n], op0=ALU.mult, op1=ALU.add),
             reads=[C.rps[b], r_ht[i]], writes=[r_hn[i]])
        P.dma("sp", lambda e: e.dma_start(out=src, in_=hns[i][:, 0:n]), reads=[r_hn[i]])
        return None
    return epi


def phase_input(C):
    P, A = C.P, C.A
    phase_begin(C)
    xin = [A.alloc([D], F32) for _ in range(2)]
    hst = [A.alloc([KD, 128], F32) for _ in range(2)]
    r_x = [Res(), Res()]
    r_h = [Res(), Res()]
    hTv = kview(C.hT)
    bi = 0

    def in_load(tc):
        s = tc % 2
        src = C.ctx_in[tc * 128:(tc + 1) * 128, :] if tc < 2 else C.x_in[(tc - 2) * 128:(tc - 1) * 128, :]
        P.dma("sp", lambda e, s=s, src=src: e.dma_start(out=xin[s], in_=src), writes=[r_x[s]])

    in_load(0)
    for tc in range(T // 128):
        s = tc % 2
        if tc + 1 < T // 128:
            in_load(tc + 1)
        for g4 in range(4):
            b = bi % 8
            bi += 1

            def tr(e, s=s, g4=g4, b=b):
                for j in range(4):
                    k = g4 * 4 + j
                    i = e.transpose(C.ps[:, b, j * 128:(j + 1) * 128], xin[s][:, k * 128:(k + 1) * 128], C.ident_f)
                return i
            P.op("pe", tr, reads=[r_x[s], C.r_const], writes=[C.rps[b]])
            eng = "act" if g4 % 2 == 0 else "dve"
            dst = hst[s][:, g4 * 4:(g4 + 1) * 4, :]
            srcp = C.ps[:, b, :].rearrange("p (j t) -> p j t", j=4)
            if eng == "act":
                P.op("act", lambda e, dst=dst, srcp=srcp: e.activation(dst, srcp, AF.Identity),
                     reads=[C.rps[b]], writes=[r_h[s]])
            else:
                P.op("dve", lambda e, dst=dst, srcp=srcp: e.tensor_copy(dst, srcp),
                     reads=[C.rps[b]], writes=[r_h[s]])
        P.dma("sp", lambda e, s=s, tc=tc: e.dma_start(out=hTv[:, :, tc * 128:(tc + 1) * 128], in_=hst[s]),
              reads=[r_h[s]])


def phase_output(C):
    P, A = C.P, C.A
    phase_begin(C)
    hin = [A.alloc([KD, 128], F32) for _ in range(2)]
    ost = [A.alloc([D], F32) for _ in range(2)]
    r_i = [Res(), Res()]
    r_o = [Res(), Res()]
    hTv = kview(C.hT)
    bi = 0
    def out_load(tc):
        s = tc % 2
        t0 = NCTX + tc * 128
        P.dma("sp", lambda e, s=s, t0=t0: e.dma_start(out=hin[s], in_=hTv[:, :, t0:t0 + 128]), writes=[r_i[s]])

    out_load(0)
    for tc in range(NLAT // 128):
        s = tc % 2
        if tc + 1 < NLAT // 128:
            out_load(tc + 1)
        for g4 in range(4):
            b = bi % 8
            bi += 1

            def tr(e, s=s, g4=g4, b=b):
                for j in range(4):
                    i = e.transpose(C.ps[:, b, j * 128:(j + 1) * 128], hin[s][:, g4 * 4 + j, :], C.ident_f)
                return i
            P.op("pe", tr, reads=[r_i[s], C.r_const], writes=[C.rps[b]])
            dst = ost[s][:, g4 * 512:(g4 + 1) * 512]
            if g4 % 2 == 0:
                P.op("act", lambda e, dst=dst, b=b: e.activation(dst, C.ps[:, b, :], AF.Identity),
                     reads=[C.rps[b]], writes=[r_o[s]])
            else:
                P.op("dve", lambda e, dst=dst, b=b: e.tensor_copy(dst, C.ps[:, b, :]),
                     reads=[C.rps[b]], writes=[r_o[s]])
        P.dma("sp", lambda e, s=s, tc=tc: e.dma_start(out=C.out[tc * 128:(tc + 1) * 128, :], in_=ost[s]),
              reads=[r_o[s]])


def phase_adaln(C, l):
    P, A = C.P, C.A
    phase_begin(C)
    slots = [A.alloc([KD, 1024], BF16) for _ in range(2)]
    r_s = [Res(), Res()]
    tmp = A.alloc([16, 2], F32)
    r_tmp = Res()
    Wv = kview(C.mod_w[l])
    for s8 in range(12):
        s = s8 % 2
        P.dma("pool", lambda e, s=s, s8=s8: e.dma_start(out=slots[s], in_=Wv[:, :, s8 * 1024:(s8 + 1) * 1024]),
              writes=[r_s[s]])
        b = s8 % 8

        def mm(e, s=s, b=b):
            for c in range(8):
                for k in range(KD):
                    i = e.matmul(C.ps[:, b, 2 * c:2 * c + 2], slots[s][:, k, c * 128:(c + 1) * 128], C.sc_b[:, k, :],
                                 start=(k == 0), stop=(k == KD - 1))
            return i
        P.op("pe", mm, reads=[r_s[s], C.r_const], writes=[C.rps[b]])
        for col in range(2):
            P.op("dve", lambda e, b=b, col=col, s8=s8: e.tensor_tensor(
                out=C.mod[:, s8 * 8:(s8 + 1) * 8, col],
                in0=C.ps[:, b, 0:16].rearrange("p (c two) -> p c two", two=2)[:, :, col],
                in1=C.modb[:, l, s8 * 8:(s8 + 1) * 8], op=ALU.add),
                reads=[C.rps[b], C.r_const], writes=[C.r_mod])
    for sub in range(2):
        base = 48 * sub
        P.op("dve", lambda e, base=base: e.tensor_scalar(out=tmp, in0=C.mod[:, base + 16:base + 32, :], scalar1=1.0,
                                                         scalar2=None, op0=ALU.add),
             reads=[C.r_mod], writes=[r_tmp])
        for col in range(2):
            P.op("dve", lambda e, sub=sub, col=col: e.tensor_tensor(out=C.gs[:, sub, :, col], in0=tmp[:, :, col],
                                                                    in1=C.normg[:, l, sub, :], op=ALU.mult),
                 reads=[r_tmp, C.r_const], writes=[C.r_mod])


def phase_norm(C, l, sub, with_ctx=True):
    P, A = C.P, C.A
    phase_begin(C)
    xt = [A.alloc([KD, 512], F32) for _ in range(2)]
    sq = [A.alloc([KD, 512], BF16) for _ in range(2)]
    at = [A.alloc([KD, 512], BF16) for _ in range(2)]
    xn = [A.alloc([512], F32) for _ in range(4)]
    rs = [A.alloc([512], F32) for _ in range(2)]
    rstd = [A.alloc([512], F32) for _ in range(2)]
    r_xt, r_sq, r_at = [Res(), Res()], [Res(), Res()], [Res(), Res()]
    r_xn = [Res() for _ in range(4)]
    r_rs, r_rstd = [Res(), Res()], [Res(), Res()]
    hTv = kview(C.hT)
    aTv = kview(C.aT)
    base = 48 * sub
    xi = 0
    tiles = TILES if with_ctx else TILES[1:]

    def n_load(ti):
        off, n = tiles[ti]
        s = ti % 2
        P.dma("sp", lambda e, s=s, off=off, n=n: e.dma_start(out=xt[s][:, :, 0:n], in_=hTv[:, :, off:off + n]),
              writes=[r_xt[s]])

    n_load(0)
    for ti, (off, n) in enumerate(tiles):
        s = ti % 2
        b = ti % 8
        col = 1 if off < NCTX else 0
        if ti + 1 < len(tiles):
            n_load(ti + 1)
        P.op("act", lambda e, s=s, n=n: e.activation(sq[s][:, :, 0:n], xt[s][:, :, 0:n], AF.Square),
             reads=[r_xt[s]], writes=[r_sq[s]])

        def mm(e, s=s, b=b, n=n):
            for k in range(KD):
                i = e.matmul(C.ps[:, b, 0:n], C.ones_b, sq[s][:, k, 0:n], start=(k == 0), stop=(k == KD - 1))
            return i
        P.op("pe", mm, reads=[r_sq[s], C.r_const], writes=[C.rps[b]])
        P.op("act", lambda e, s=s, b=b, n=n: e.activation(rs[s][:, 0:n], C.ps[:, b, 0:n], AF.Sqrt,
                                                          scale=1.0 / D, bias=C.eps_t[:, 0:1]),
             reads=[C.rps[b], C.r_const], writes=[r_rs[s]])
        P.op("dve", lambda e, s=s, n=n: e.reciprocal(rstd[s][:, 0:n], rs[s][:, 0:n]),
             reads=[r_rs[s]], writes=[r_rstd[s]])
        for k in range(KD):
            x4 = xi % 4
            xi += 1
            P.op("dve", lambda e, s=s, k=k, n=n, x4=x4: e.tensor_tensor(out=xn[x4][:, 0:n], in0=xt[s][:, k, 0:n],
                                                                        in1=rstd[s][:, 0:n], op=ALU.mult),
                 reads=[r_xt[s], r_rstd[s]], writes=[r_xn[x4]])
            P.op("act", lambda e, s=s, k=k, n=n, x4=x4, col=col: e.activation(
                at[s][:, k, 0:n], xn[x4][:, 0:n], AF.Identity,
                scale=C.gs[:, sub, k, col:col + 1], bias=C.mod[:, base + k, col:col + 1]),
                reads=[r_xn[x4], C.r_mod], writes=[r_at[s]])
        P.dma("sp", lambda e, s=s, off=off, n=n: e.dma_start(out=aTv[:, :, off:off + n], in_=at[s][:, :, 0:n]),
              reads=[r_at[s]])


def phase_ffn(C, l, with_ctx=True):
    P, A = C.P, C.A
    phase_begin(C)
    W = C.ffn_w_in[l]
    cw = A.alloc([88, 3], F32)
    cb = A.alloc([88], F32)
    r_cw = Res()
    P.dma("sp", lambda e: e.dma_start(out=cw, in_=C.convw[:, l, :, :]), writes=[r_cw])
    P.dma("sp", lambda e: e.dma_start(out=cb, in_=C.convb[:, l, :]), writes=[r_cw])
    RL = 2432 + 8
    urow = [A.alloc([RL], F32) for _ in range(2)]
    acc = A.alloc([RL], F32)
    accb = A.alloc([RL], F32)
    r_accb = Res()
    grow = [A.alloc([RL], BF16) for _ in range(2)]
    r_u = [Res(), Res()]
    r_acc = Res()
    r_g = [Res(), Res()]
    groups = [
        ([TILES[0], TILES[1], TILES[2], TILES[3], TILES[4], (2304, 128)], 0, 0, 2304),
        ([(2176, 128), TILES[5], TILES[6], TILES[7], TILES[8]], 2176, 2304, 4352),
    ]
    if not with_ctx:
        groups[0] = ([TILES[1], TILES[2], TILES[3], TILES[4], (2304, 128)], NCTX, NCTX, 2304)
    gcount = [0]
    for (g, gstart, own0, own1) in groups:
        for u_ in urow:
            P.op("pool", lambda e, u_=u_: e.memset(u_, 0.0), writes=[r_u[0], r_u[1]])

        def ridx(t, gstart=gstart):
            return (t - gstart) + 1 + (2 if (t >= NCTX and gstart < NCTX) else 0)

        def epi(si, ui, tile, banks, ti, nt, ridx=ridx, own0=own0, own1=own1, g=g):
            off, n = tile
            j = si * 2 + ui
            r0 = ridx(off)
            P.op("act", lambda e: e.activation(urow[0][:, r0:r0 + n], C.ps[:, banks[0], 0:n], AF.Identity),
                 reads=[C.rps[banks[0]]], writes=[r_u[0]])
            P.op("dve", lambda e: e.tensor_copy(urow[1][:, r0:r0 + n], C.ps[:, banks[1], 0:n]),
                 reads=[C.rps[banks[1]]], writes=[r_u[1]])
            if ti != nt - 1:
                return None
            L = ridx(g[-1][0]) + g[-1][1]
            gi = gcount[0] % 2
            gcount[0] += 1

            def conv(eng, dst, src, ch, rdst, rsrc):
                P.op("dve", lambda e: e.tensor_scalar(out=dst[:, 1:L], in0=src[:, 0:L - 1], scalar1=cw[:, ch, 0:1],
                                                    scalar2=cb[:, ch:ch + 1], op0=ALU.mult, op1=ALU.add),
                     reads=[rsrc, r_cw], writes=[rdst])
                P.op("dve", lambda e: e.scalar_tensor_tensor(out=dst[:, 1:L], in0=src[:, 1:L], scalar=cw[:, ch, 1:2],
                                                             in1=dst[:, 1:L], op0=ALU.mult, op1=ALU.add),
                     reads=[rsrc, r_cw], writes=[rdst])
                P.op("dve", lambda e: e.scalar_tensor_tensor(out=dst[:, 1:L], in0=src[:, 2:L + 1], scalar=cw[:, ch, 2:3],
                                                             in1=dst[:, 1:L], op0=ALU.mult, op1=ALU.add),
                     reads=[rsrc, r_cw], writes=[rdst])
            conv("dve", acc, urow[0], j, r_acc, r_u[0])
            conv("dve", accb, urow[1], 44 + j, r_accb, r_u[1])
            P.op("act", lambda e: e.activation(acc[:, 1:L], acc[:, 1:L], AF.Silu), reads=[r_acc], writes=[r_acc])
            P.op("pool", lambda e: e.tensor_tensor(out=grow[gi][:, 1:L], in0=acc[:, 1:L], in1=accb[:, 1:L],
                                                   op=ALU.mult),
                 reads=[r_acc, r_accb], writes=[r_g[gi]])
            segs = []
            if own0 < NCTX:
                segs.append((0, NCTX))
                segs.append((NCTX, own1))
            else:
                segs.append((own0, own1))
            for (a0, a1) in segs:
                ra = ridx(a0)
                P.dma("sp", lambda e, a0=a0, a1=a1, ra=ra: e.dma_start(
                    out=C.gT[j * 128:(j + 1) * 128, a0:a1], in_=grow[gi][:, ra:ra + (a1 - a0)]), reads=[r_g[gi]])
            return None

        supers = []
        for sc in range(22):
            pieces = [(W, sc * 256, 256), (W, DFF + sc * 256, 256)]
            units = [[0, 256], [128, 384]]
            supers.append((pieces, units))
        mark = A.off
        gemm(C, C.aT, KD, [g], [dict(kind="B", supers=supers, epi=epi)], list(range(8)), 512)
        C.P.barrier()
        A.off = mark
    phase_begin(C)
    Wo = C.ffn_w_out[l]
    epi = make_resid_epi(C, lambda c, col: C.mod[:, 80 + c, col:col + 1], lambda si, ui: si * 2 + ui)
    supers = [([(Wo, sc * 256, 256)], [[0], [128]]) for sc in range(8)]
    groups = [[TILES[0], TILES[1], TILES[2]], [TILES[3], TILES[4]], [TILES[5], TILES[6]], [TILES[7], TILES[8]]]
    if not with_ctx:
        groups[0] = [TILES[1], TILES[2]]
    gemm(C, C.gT, 44, groups, [dict(kind="B", supers=supers, epi=epi)], list(range(8)), 256)


def make_qk_epi(C, gvec_of, dst, chunk_of, cosT, sinT, r_tab):
    P, A = C.P, C.A
    NB = 2
    sqb = [A.alloc([512], BF16) for _ in range(NB)]
    gqf = [A.alloc([512], F32) for _ in range(NB)]
    gqb = [A.alloc([512], BF16) for _ in range(NB)]
    rs = [A.alloc([512], F32) for _ in range(NB)]
    rstd = [A.alloc([512], F32) for _ in range(NB)]
    t1 = [A.alloc([512], F32) for _ in range(NB)]
    t2 = [A.alloc([512], F32) for _ in range(NB)]
    ob = [A.alloc([512], BF16) for _ in range(NB)]
    R = lambda: [Res() for _ in range(NB)]
    r_sq, r_gf, r_gb, r_rs, r_rstd, r_t1, r_t2, r_ob = R(), R(), R(), R(), R(), R(), R(), R()
    cnt = [0]

    def epi(si, ui, tile, banks, ti, nt):
        off, n = tile
        ch = chunk_of(si, ui)
        i = cnt[0] % NB
        cnt[0] += 1
        b = banks[0]
        bs = 4 + i
        br = 6 + i
        lat = off >= NCTX
        P.op("act", lambda e: e.activation(sqb[i][:, 0:n], C.ps[:, b, 0:n], AF.Square),
             reads=[C.rps[b]], writes=[r_sq[i]])
        P.op("act", lambda e: e.activation(gqf[i][:, 0:n], C.ps[:, b, 0:n], AF.Identity, scale=gvec_of(ch)),
             reads=[C.rps[b], C.r_lay], writes=[r_gf[i]])
        if lat:
            P.op("pool", lambda e: e.tensor_copy(gqb[i][:, 0:n], gqf[i][:, 0:n]), reads=[r_gf[i]], writes=[r_gb[i]])

        def later():
            P.op("pe", lambda e: e.matmul(C.ps[:, bs, 0:n], C.ones_b, sqb[i][:, 0:n], start=True, stop=True),
                 reads=[r_sq[i], C.r_const], writes=[C.rps[bs]])
            if lat:
                P.op("pe", lambda e: e.matmul(C.ps[:, br, 0:n], C.rotm_b, gqb[i][:, 0:n], start=True, stop=True),
                     reads=[r_gb[i], C.r_const], writes=[C.rps[br]])
            P.op("act", lambda e: e.activation(rs[i][:, 0:n], C.ps[:, bs, 0:n], AF.Ln, scale=1.0 / 128,
                                               bias=C.eps_t[:, 0:1]),
                 reads=[C.rps[bs], C.r_const], writes=[r_rs[i]])
            P.op("act", lambda e: e.activation(rstd[i][:, 0:n], rs[i][:, 0:n], AF.Exp, scale=-0.5),
                 reads=[r_rs[i]], writes=[r_rstd[i]])
            if lat:
                lo = off - NCTX
                P.op("dve", lambda e: e.tensor_tensor(out=t1[i][:, 0:n], in0=gqf[i][:, 0:n], in1=cosT[:, lo:lo + n],
                                                      op=ALU.mult), reads=[r_gf[i], r_tab], writes=[r_t1[i]])
                P.op("dve", lambda e: e.tensor_tensor(out=t2[i][:, 0:n], in0=C.ps[:, br, 0:n], in1=sinT[:, lo:lo + n],
                                                      op=ALU.mult), reads=[C.rps[br], r_tab], writes=[r_t2[i]])
                P.op("pool", lambda e: e.tensor_tensor(out=t1[i][:, 0:n], in0=t1[i][:, 0:n], in1=t2[i][:, 0:n],
                                                       op=ALU.add), reads=[r_t2[i]], writes=[r_t1[i]])
                P.op("dve", lambda e: e.tensor_tensor(out=ob[i][:, 0:n], in0=t1[i][:, 0:n], in1=rstd[i][:, 0:n],
                                                      op=ALU.mult), reads=[r_t1[i], r_rstd[i]], writes=[r_ob[i]])
            else:
                P.op("dve", lambda e: e.tensor_tensor(out=ob[i][:, 0:n], in0=gqf[i][:, 0:n], in1=rstd[i][:, 0:n],
                                                      op=ALU.mult), reads=[r_gf[i], r_rstd[i]], writes=[r_ob[i]])
            P.dma("sp", lambda e: e.dma_start(out=dst[ch * 128:(ch + 1) * 128, off:off + n], in_=ob[i][:, 0:n]),
                  reads=[r_ob[i]])
        return later
    return epi


def make_v_epi(C, dst, ncols):
    P, A = C.P, C.A
    vst = [A.alloc([512], BF16) for _ in range(3)]
    r_v = [Res() for _ in range(3)]
    cnt = [0]

    def epi(fi, tok0, bank):
        i = cnt[0] % 3
        cnt[0] += 1
        if i % 2 == 0:
            P.op("act", lambda e: e.activation(vst[i], C.ps[:, bank, :], AF.Identity),
                 reads=[C.rps[bank]], writes=[r_v[i]])
        else:
            P.op("dve", lambda e: e.tensor_copy(vst[i], C.ps[:, bank, :]), reads=[C.rps[bank]], writes=[r_v[i]])
        P.dma("sp", lambda e: e.dma_start(out=dst[tok0:tok0 + 128, fi * 512:(fi + 1) * 512], in_=vst[i]),
              reads=[r_v[i]])
        return None
    return epi


HALF_GROUPS = [[TILES[0], TILES[1], TILES[2], TILES[3], TILES[4]], [TILES[5], TILES[6], TILES[7], TILES[8]]]


def load_rope_tables(C, cos_d, sin_d):
    P, A = C.P, C.A
    cosT = A.alloc([NLAT], F32)
    sinT = A.alloc([NLAT], F32)
    r_tab = Res()
    P.dma("sp", lambda e: e.dma_start(out=cosT, in_=cos_d), writes=[r_tab])
    P.dma("sp", lambda e: e.dma_start(out=sinT, in_=sin_d), writes=[r_tab])
    return cosT, sinT, r_tab


def phase_ga(C, l, j, with_ctx=True):
    P, A = C.P, C.A
    W = C.ga_wqkv[j]
    phase_begin(C)
    cosT, sinT, r_tab = load_rope_tables(C, C.cos_ax, C.sin_ax)
    gv = A.alloc([2], F32)
    P.dma("sp", lambda e: e.dma_start(out=gv, in_=C.ga_qkg[:, j, :]), writes=[C.r_lay])
    P.op("dve", lambda e: e.tensor_scalar(out=gv[:, 0:1], in0=gv[:, 0:1], scalar1=128 ** -0.5, scalar2=None,
                                          op0=ALU.mult), reads=[C.r_lay], writes=[C.r_lay])
    qk_epi = make_qk_epi(C, lambda ch: gv[:, 0:1] if ch < 16 else gv[:, 1:2], C.qkT,
                         lambda si, ui: si * 4 + ui, cosT, sinT, r_tab)
    v_epi = make_v_epi(C, C.vtok, 512)
    supers = [([(W, sc * 512, 512)], [[0], [128], [256], [384]]) for sc in range(5)]
    gemm(C, C.aT, KD, HALF_GROUPS,
         [dict(kind="B", supers=supers, epi=qk_epi), dict(kind="A", cols=[(W, 2560)], epi=v_epi)],
         [0, 1, 2, 3], 512)
    phase_begin(C)
    sk = A.alloc([16], F32)
    es16 = A.alloc([16], F32)
    zero = A.alloc([128], F32)
    esf = A.alloc([16, 128], F32)
    mprev = A.alloc([4, 128], BF16)
    mnext = A.alloc([4, 128], BF16)
    r_m = Res()
    P.dma("sp", lambda e: e.dma_start(out=sk, in_=C.ga_sink[j].partition_broadcast(128)), writes=[r_m])
    P.dma("pool", lambda e: e.dma_start(out=mprev, in_=C.mask_prev), writes=[r_m])
    P.dma("pool", lambda e: e.dma_start(out=mnext, in_=C.mask_next), writes=[r_m])
    P.op("act", lambda e: e.activation(es16, sk, AF.Exp), reads=[r_m], writes=[r_m])
    P.op("dve", lambda e: e.memset(zero, 0.0), writes=[r_m])
    for h in range(16):
        P.op("dve", lambda e, h=h: e.tensor_scalar(out=esf[:, h, :], in0=zero, scalar1=es16[:, h:h + 1], scalar2=None,
                                                   op0=ALU.add), reads=[r_m], writes=[r_m])
    Ksb = [A.alloc([T], BF16) for _ in range(2)]
    Vsb = [A.alloc([34, 128], BF16) for _ in range(2)]
    Qsb = [A.alloc([4, T], BF16) for _ in range(2)]
    r_kvq = [Res(), Res()]
    NP = 3
    pt = [A.alloc([4, 128], BF16) for _ in range(NP)]
    r_pt = [Res() for _ in range(NP)]
    dn = [A.alloc([4, 128], F32) for _ in range(2)]
    rd = [A.alloc([4, 128], F32) for _ in range(2)]
    r_dn, r_rd = [Res(), Res()], [Res(), Res()]
    ost = [A.alloc([4, 512], BF16) for _ in range(2)]
    r_ost = [Res(), Res()]
    qkv = C.qkT
    oTv = kview(C.oT)
    step_i = [0]
    unit_i = [0]
    stage_i = [0]
    def ga_load(g):
        s = g % 2
        P.dma("sp", lambda e, s=s, g=g: e.dma_start(out=Ksb[s], in_=qkv[(16 + g) * 128:(17 + g) * 128, :]),
              writes=[r_kvq[s]])
        P.dma("sp", lambda e, s=s, g=g: e.dma_start(
            out=Vsb[s], in_=C.vtok[:, g * 128:(g + 1) * 128].rearrange("(c p) d -> p c d", p=128)),
            writes=[r_kvq[s]])
        P.dma("sp", lambda e, s=s, g=g: e.dma_start(
            out=Qsb[s], in_=kview(qkv[g * 512:(g + 1) * 512, :])), writes=[r_kvq[s]])

    ga_load(0)
    for g in range(4):
        s = g % 2
        if g + 1 < 4:
            ga_load(g + 1)
        units = []
        for qc in range(2 if with_ctx else 0):
            units.append((qc, [(0, None), (1, None)]))
        for c in range(32):
            ks = [(0, None), (1, None)]
            if c > 0:
                ks.append((2 + c - 1, "prev"))
            ks.append((2 + c, None))
            if c < 31:
                ks.append((2 + c + 1, "next"))
            units.append((2 + c, ks))
        steps = []
        for ui, (qc, ks) in enumerate(units):
            for ki, (kc, mk) in enumerate(ks):
                steps.append((ui, qc, kc, mk, ki == 0, ki == len(ks) - 1))
        pend = None

        def s_mm(st, s=s):
            ui, qc, kc, mk, first, last = st
            sb = step_i[0] % 3
            pi = step_i[0] % NP
            step_i[0] += 1
            qm = Qsb[s][:, :, qc * 128:(qc + 1) * 128]
            outp = C.ps[:, sb, :].rearrange("p (h q) -> p h q", h=4)
            P.op("pe", lambda e: e.matmul(outp, Ksb[s][:, kc * 128:(kc + 1) * 128], qm, start=True, stop=True),
                 reads=[r_kvq[s]], writes=[C.rps[sb]])
            P.op("act", lambda e: e.activation(pt[pi], outp, AF.Exp), reads=[C.rps[sb]], writes=[r_pt[pi]])
            if mk is not None:
                m = mprev if mk == "prev" else mnext
                P.op("dve", lambda e: e.tensor_tensor(out=pt[pi], in0=pt[pi], in1=m, op=ALU.mult),
                     reads=[r_m], writes=[r_pt[pi]])
            return pi

        def pv(st, pi, s=s, g=g):
            ui, qc, kc, mk, first, last = st
            if first:
                unit_i[0] += 1
            u2 = unit_i[0] % 2
            bo, bd = 3 + u2, 5 + u2
            po = C.ps[:, bo, :].rearrange("p (h q) -> p h q", h=4)
            pd = C.ps[:, bd, :].rearrange("p (h q) -> p h q", h=4)

            def mm(e):
                e.matmul(po, Vsb[s][:, kc, :], pt[pi], start=first, stop=last)
                return e.matmul(pd, C.ones_b, pt[pi], start=first, stop=last)
            P.op("pe", mm, reads=[r_pt[pi], r_kvq[s], C.r_const], writes=[C.rps[bo], C.rps[bd]])
            if not last:
                return
            P.op("dve", lambda e: e.tensor_tensor(out=dn[u2], in0=pd, in1=esf[:, 4 * g:4 * g + 4, :], op=ALU.add),
                 reads=[C.rps[bd], r_m], writes=[r_dn[u2]])
            P.op("act", lambda e: e.activation(dn[u2], dn[u2], AF.Ln), reads=[r_dn[u2]], writes=[r_dn[u2]])
            P.op("act", lambda e: e.activation(rd[u2], dn[u2], AF.Exp, scale=-1.0), reads=[r_dn[u2]], writes=[r_rd[u2]])
            if qc < 2:
                so, flush, t0, tn = qc * 128, qc == 1, 0, 256
            else:
                c = qc - 2
                so, flush, t0, tn = (c % 4) * 128, c % 4 == 3, NCTX + (c // 4) * 512, 512
            si_ = stage_i[0] % 2
            P.op("dve", lambda e: e.tensor_tensor(out=ost[si_][:, :, so:so + 128], in0=po, in1=rd[u2], op=ALU.mult),
                 reads=[C.rps[bo], r_rd[u2]], writes=[r_ost[si_]])
            if flush:
                P.dma("sp", lambda e: e.dma_start(out=oTv[:, 4 * g:4 * g + 4, t0:t0 + tn], in_=ost[si_][:, :, 0:tn]),
                      reads=[r_ost[si_]])
                stage_i[0] += 1

        for st in steps:
            pi = s_mm(st)
            if pend is not None:
                pv(*pend)
            pend = (st, pi)
        pv(*pend)
    phase_begin(C)
    Wo = C.ga_wo[j]
    epi = make_resid_epi(C, lambda c, col: C.mod[:, 32 + c, col:col + 1], lambda si, ui: si * 4 + ui)
    supers = [([(Wo, sc * 512, 512)], [[0], [128], [256], [384]]) for sc in range(4)]
    wo_groups = HALF_GROUPS if with_ctx else [HALF_GROUPS[0][1:], HALF_GROUPS[1]]
    gemm(C, C.oT[0:D, :], KD, wo_groups, [dict(kind="B", supers=supers, epi=epi)], list(range(8)), 512)


def phase_df(C, l):
    P, A = C.P, C.A
    W = C.df_wqkv[0]
    lam_init = 0.8 - 0.6 * math.exp(-0.3 * l)
    phase_begin(C)
    cosT, sinT, r_tab = load_rope_tables(C, C.cos_ax, C.sin_ax)
    gv = A.alloc([2], F32)
    P.dma("sp", lambda e: e.dma_start(out=gv, in_=C.df_qkg), writes=[C.r_lay])
    P.op("dve", lambda e: e.tensor_scalar(out=gv[:, 0:1], in0=gv[:, 0:1], scalar1=128 ** -0.5, scalar2=None,
                                          op0=ALU.mult), reads=[C.r_lay], writes=[C.r_lay])
    qk_epi = make_qk_epi(C, lambda ch: gv[:, 0:1] if ch < 16 else gv[:, 1:2], C.qkT,
                         lambda si, ui: si * 4 + ui, cosT, sinT, r_tab)
    v_epi = make_v_epi(C, C.vtok, 2048)
    supers = [([(W, sc * 512, 512)], [[0], [128], [256], [384]]) for sc in range(8)]
    gemm(C, C.aT, KD, HALF_GROUPS,
         [dict(kind="B", supers=supers, epi=qk_epi),
          dict(kind="A", cols=[(W, 4096 + 512 * i) for i in range(4)], epi=v_epi)],
         [0, 1, 2, 3], 512)
    phase_begin(C)
    lam_t = A.alloc([4, 128], F32)
    pr = A.alloc([2, 128], F32)
    sm = A.alloc([2], F32)
    nlam = A.alloc([1], F32)
    r_l = Res()
    P.dma("sp", lambda e: e.dma_start(out=lam_t, in_=C.df_lam.rearrange("a d -> (a d)").partition_broadcast(128)),
          writes=[r_l])
    for i2 in range(2):
        P.op("dve", lambda e, i2=i2: e.tensor_tensor(out=pr[:, i2, :], in0=lam_t[:, 2 * i2, :],
                                                     in1=lam_t[:, 2 * i2 + 1, :], op=ALU.mult),
             reads=[r_l], writes=[r_l])
        P.op("dve", lambda e, i2=i2: e.tensor_reduce(out=sm[:, i2:i2 + 1], in_=pr[:, i2, :],
                                                     axis=mybir.AxisListType.X, op=ALU.add),
             reads=[r_l], writes=[r_l])
    P.op("act", lambda e: e.activation(sm, sm, AF.Exp), reads=[r_l], writes=[r_l])
    P.op("dve", lambda e: e.tensor_tensor(out=nlam, in0=sm[:, 1:2], in1=sm[:, 0:1], op=ALU.subtract),
         reads=[r_l], writes=[r_l])
    P.op("dve", lambda e: e.tensor_scalar(out=nlam, in0=nlam, scalar1=-lam_init, scalar2=None, op0=ALU.add),
         reads=[r_l], writes=[r_l])
    Ksb2 = [A.alloc([2, T], BF16) for _ in range(2)]
    Qsb2 = [A.alloc([2, T], BF16) for _ in range(2)]
    Vsb2 = [A.alloc([34, 256], BF16) for _ in range(2)]
    r_k2, r_q2, r_v2 = [Res(), Res()], [Res(), Res()], [Res(), Res()]
    pt0 = [A.alloc([512], BF16) for _ in range(2)]
    pt1 = [A.alloc([512], BF16) for _ in range(2)]
    r_pt = [Res(), Res()]
    r0t = A.alloc([512], F32)
    r1t = A.alloc([512], F32)
    t0t = A.alloc([512], F32)
    t1t = A.alloc([512], F32)
    r_r, r_t = Res(), Res()
    of = [A.alloc([2, 512], F32) for _ in range(2)]
    r_of = [Res(), Res()]
    oFv = kview(C.oF)
    step_i = [0]
    unit_n = [0]
    def df_load(h):
        b2 = h % 2
        P.dma("sp", lambda e: e.dma_start(out=Ksb2[b2], in_=kview(C.qkT[(16 + 2 * h) * 128:(18 + 2 * h) * 128, :])),
              writes=[r_k2[b2]])
        P.dma("sp", lambda e: e.dma_start(out=Qsb2[b2], in_=kview(C.qkT[(2 * h) * 128:(2 * h + 2) * 128, :])),
              writes=[r_q2[b2]])
        P.dma("sp", lambda e: e.dma_start(
            out=Vsb2[b2], in_=C.vtok[:, h * 256:(h + 1) * 256].rearrange("(c p) d -> p c d", p=128)),
            writes=[r_v2[b2]])

    df_load(0)
    for h in range(8):
        if h + 1 < 8:
            df_load(h + 1)
        Ksb, Qsb, Vsb = Ksb2[h % 2], Qsb2[h % 2], Vsb2[h % 2]
        r_k, r_q, r_v = r_k2[h % 2], r_q2[h % 2], r_v2[h % 2]
        units = [((0, 256), [0, 1])] + [(TILES[i], list(range(34))) for i in range(1, 9)]
        steps = []
        for (tile, ks) in units:
            for ki, kc in enumerate(ks):
                steps.append((tile, kc, ki == 0, ki == len(ks) - 1))

        def s_step(st, Ksb=Ksb, Qsb=Qsb, r_k=r_k, r_q=r_q):
            (off, n), kc, first, last = st
            sl = step_i[0] % 2
            step_i[0] += 1

            def mm(e):
                e.matmul(C.ps[:, 0, 0:n], Ksb[:, 0, kc * 128:(kc + 1) * 128], Qsb[:, 0, off:off + n],
                         start=True, stop=True)
                return e.matmul(C.ps[:, 1, 0:n], Ksb[:, 1, kc * 128:(kc + 1) * 128], Qsb[:, 1, off:off + n],
                                start=True, stop=True)
            P.op("pe", mm, reads=[r_k, r_q], writes=[C.rps[0], C.rps[1]])
            P.op("act", lambda e: e.activation(pt0[sl][:, 0:n], C.ps[:, 0, 0:n], AF.Exp),
                 reads=[C.rps[0]], writes=[r_pt[sl]])
            P.op("act", lambda e: e.activation(pt1[sl][:, 0:n], C.ps[:, 1, 0:n], AF.Exp),
                 reads=[C.rps[1]], writes=[r_pt[sl]])
            return sl

        def pv_step(st, sl, h=h, Vsb=Vsb, r_v=r_v):
            (off, n), kc, first, last = st

            def mm(e):
                for e2 in range(2):
                    e.matmul(C.ps[:, 2 + e2, 0:n], Vsb[:, kc, e2 * 128:(e2 + 1) * 128], pt0[sl][:, 0:n],
                             start=first, stop=last)
                e.matmul(C.ps[:, 6, 0:n], C.ones_b, pt0[sl][:, 0:n], start=first, stop=last)
                for e2 in range(2):
                    e.matmul(C.ps[:, 4 + e2, 0:n], Vsb[:, kc, e2 * 128:(e2 + 1) * 128], pt1[sl][:, 0:n],
                             start=first, stop=last)
                return e.matmul(C.ps[:, 7, 0:n], C.ones_b, pt1[sl][:, 0:n], start=first, stop=last)
            P.op("pe", mm, reads=[r_pt[sl], r_v, C.r_const], writes=[C.rps[b] for b in range(2, 8)])
            if not last:
                return
            ui = unit_n[0] % 2
            unit_n[0] += 1
            P.op("act", lambda e: e.activation(r0t[:, 0:n], C.ps[:, 6, 0:n], AF.Ln), reads=[C.rps[6]], writes=[r_r])
            P.op("act", lambda e: e.activation(r1t[:, 0:n], C.ps[:, 7, 0:n], AF.Ln), reads=[C.rps[7]], writes=[r_r])
            P.op("act", lambda e: e.activation(r0t[:, 0:n], r0t[:, 0:n], AF.Exp, scale=-1.0), reads=[r_r], writes=[r_r])
            P.op("act", lambda e: e.activation(r1t[:, 0:n], r1t[:, 0:n], AF.Exp, scale=-1.0), reads=[r_r], writes=[r_r])
            for e2 in range(2):
                P.op("dve", lambda e, e2=e2: e.tensor_tensor(out=t0t[:, 0:n], in0=C.ps[:, 2 + e2, 0:n],
                                                             in1=r0t[:, 0:n], op=ALU.mult),
                     reads=[C.rps[2 + e2], r_r], writes=[r_t])
                P.op("dve", lambda e, e2=e2: e.tensor_tensor(out=t1t[:, 0:n], in0=C.ps[:, 4 + e2, 0:n],
                                                             in1=r1t[:, 0:n], op=ALU.mult),
                     reads=[C.rps[4 + e2], r_r], writes=[r_t])
                P.op("dve", lambda e, e2=e2: e.scalar_tensor_tensor(out=of[ui][:, e2, 0:n], in0=t1t[:, 0:n],
                                                                    scalar=nlam[:, 0:1], in1=t0t[:, 0:n],
                                                                    op0=ALU.mult, op1=ALU.add),
                     reads=[r_t, r_l], writes=[r_of[ui]])
            P.dma("sp", lambda e: e.dma_start(out=oFv[:, 2 * h:2 * h + 2, off:off + n], in_=of[ui][:, :, 0:n]),
                  reads=[r_of[ui]])

        pend = None
        for st in steps:
            sl = s_step(st)
            if pend is not None:
                pv_step(*pend)
            pend = (st, sl)
        pv_step(*pend)
    phase_begin(C)
    sg = A.alloc([2], F32)
    r_sg = Res()
    P.dma("sp", lambda e: e.dma_start(out=sg, in_=C.df_subln), writes=[r_sg])
    P.op("dve", lambda e: e.tensor_scalar(out=sg, in0=sg, scalar1=1.0 - lam_init, scalar2=None, op0=ALU.mult),
         reads=[r_sg], writes=[r_sg])
    ot = [A.alloc([KD, 512], F32) for _ in range(2)]
    sq = [A.alloc([KD, 512], BF16) for _ in range(2)]
    at = [A.alloc([KD, 512], BF16) for _ in range(2)]
    r_ot, r_sq, r_at = [Res(), Res()], [Res(), Res()], [Res(), Res()]
    rs = [A.alloc([512], F32) for _ in range(2)]
    rstd = [A.alloc([512], F32) for _ in range(2)]
    xn = [A.alloc([512], F32) for _ in range(4)]
    r_rs, r_rstd = [Res(), Res()], [Res(), Res()]
    r_xn = [Res() for _ in range(4)]
    o2v = kview(C.oT[0:D, :])
    xi = 0
    hi = 0
    for ti, (off, n) in enumerate(TILES):
        s = ti % 2
        P.dma("sp", lambda e, s=s, off=off, n=n: e.dma_start(out=ot[s][:, :, 0:n], in_=oFv[:, :, off:off + n]),
              writes=[r_ot[s]])
        P.op("act", lambda e, s=s, n=n: e.activation(sq[s][:, :, 0:n], ot[s][:, :, 0:n], AF.Square),
             reads=[r_ot[s]], writes=[r_sq[s]])
        for h in range(8):
            def mm(e, s=s, n=n, h=h):
                e.matmul(C.ps[:, h, 0:n], C.ones_b, sq[s][:, 2 * h, 0:n], start=True, stop=False)
                return e.matmul(C.ps[:, h, 0:n], C.ones_b, sq[s][:, 2 * h + 1, 0:n], start=False, stop=True)
            P.op("pe", mm, reads=[r_sq[s], C.r_const], writes=[C.rps[h]])
        for h in range(8):
            h2 = hi % 2
            hi += 1
            P.op("act", lambda e, h=h, h2=h2, n=n: e.activation(rs[h2][:, 0:n], C.ps[:, h, 0:n], AF.Sqrt,
                                                                scale=1.0 / 256, bias=C.eps_t[:, 0:1]),
                 reads=[C.rps[h], C.r_const], writes=[r_rs[h2]])
            P.op("dve", lambda e, h2=h2, n=n: e.reciprocal(rstd[h2][:, 0:n], rs[h2][:, 0:n]),
                 reads=[r_rs[h2]], writes=[r_rstd[h2]])
            for e2 in range(2):
                x4 = xi % 4
                xi += 1
                k = 2 * h + e2
                P.op("dve", lambda e, s=s, k=k, n=n, x4=x4, h2=h2: e.tensor_tensor(
                    out=xn[x4][:, 0:n], in0=ot[s][:, k, 0:n], in1=rstd[h2][:, 0:n], op=ALU.mult),
                    reads=[r_ot[s], r_rstd[h2]], writes=[r_xn[x4]])
                P.op("act", lambda e, s=s, k=k, n=n, x4=x4, e2=e2: e.activation(
                    at[s][:, k, 0:n], xn[x4][:, 0:n], AF.Identity, scale=sg[:, e2:e2 + 1]),
                    reads=[r_xn[x4], r_sg], writes=[r_at[s]])
        P.dma("sp", lambda e, s=s, off=off, n=n: e.dma_start(out=o2v[:, :, off:off + n], in_=at[s][:, :, 0:n]),
              reads=[r_at[s]])
    phase_begin(C)
    Wo = C.df_wo[0]
    epi = make_resid_epi(C, lambda c, col: C.mod[:, 32 + c, col:col + 1], lambda si, ui: si * 4 + ui)
    supers = [([(Wo, sc * 512, 512)], [[0], [128], [256], [384]]) for sc in range(4)]
    gemm(C, C.oT[0:D, :], KD, HALF_GROUPS, [dict(kind="B", supers=supers, epi=epi)], list(range(8)), 512)


def phase_rt(C, l):
    P, A = C.P, C.A
    W = C.rt_w_in[0]
    phase_begin(C)
    cosT, sinT, r_tab = load_rope_tables(C, C.rt_cos, C.rt_sin)
    NB = 2
    ta = [A.alloc([512], F32) for _ in range(NB)]
    tb = [A.alloc([512], F32) for _ in range(NB)]
    tc_ = [A.alloc([512], F32) for _ in range(NB)]
    td = [A.alloc([512], F32) for _ in range(NB)]
    ob = [A.alloc([2, 512], BF16) for _ in range(NB)]
    RR = lambda: [Res() for _ in range(NB)]
    r_ta, r_tb, r_tc, r_td, r_ob = RR(), RR(), RR(), RR(), RR()
    cnt = [0]

    def qk_epi(si, ui, tile, banks, ti, nt):
        off, n = tile
        hh = si * 2 + ui
        scl = 256 ** -0.5 if hh < 8 else 1.0
        b0, b1 = banks
        i = cnt[0] % NB
        cnt[0] += 1
        if off >= NCTX:
            lo = off - NCTX
            P.op("dve", lambda e: e.tensor_tensor(out=ta[i][:, 0:n], in0=C.ps[:, b0, 0:n], in1=cosT[:, lo:lo + n],
                                                  op=ALU.mult), reads=[C.rps[b0], r_tab], writes=[r_ta[i]])
            P.op("dve", lambda e: e.tensor_tensor(out=tb[i][:, 0:n], in0=C.ps[:, b1, 0:n], in1=sinT[:, lo:lo + n],
                                                  op=ALU.mult), reads=[C.rps[b1], r_tab], writes=[r_tb[i]])
            P.op("pool", lambda e: e.tensor_tensor(out=ta[i][:, 0:n], in0=ta[i][:, 0:n], in1=tb[i][:, 0:n],
                                                   op=ALU.subtract), reads=[r_tb[i]], writes=[r_ta[i]])
            P.op("act", lambda e: e.activation(ob[i][:, 0, 0:n], ta[i][:, 0:n], AF.Identity, scale=scl),
                 reads=[r_ta[i]], writes=[r_ob[i]])
            P.op("dve", lambda e: e.tensor_tensor(out=tc_[i][:, 0:n], in0=C.ps[:, b0, 0:n], in1=sinT[:, lo:lo + n],
                                                  op=ALU.mult), reads=[C.rps[b0], r_tab], writes=[r_tc[i]])
            P.op("dve", lambda e: e.tensor_tensor(out=td[i][:, 0:n], in0=C.ps[:, b1, 0:n], in1=cosT[:, lo:lo + n],
                                                  op=ALU.mult), reads=[C.rps[b1], r_tab], writes=[r_td[i]])
            P.op("pool", lambda e: e.tensor_tensor(out=tc_[i][:, 0:n], in0=tc_[i][:, 0:n], in1=td[i][:, 0:n],
                                                   op=ALU.add), reads=[r_td[i]], writes=[r_tc[i]])
            P.op("act", lambda e: e.activation(ob[i][:, 1, 0:n], tc_[i][:, 0:n], AF.Identity, scale=scl),
                 reads=[r_tc[i]], writes=[r_ob[i]])
        else:
            P.op("act", lambda e: e.activation(ob[i][:, 0, 0:n], C.ps[:, b0, 0:n], AF.Identity, scale=scl),
                 reads=[C.rps[b0]], writes=[r_ob[i]])
            P.op("act", lambda e: e.activation(ob[i][:, 1, 0:n], C.ps[:, b1, 0:n], AF.Identity, scale=scl),
                 reads=[C.rps[b1]], writes=[r_ob[i]])
        P.dma("sp", lambda e: e.dma_start(out=kview(C.qkT[hh * 256:(hh + 1) * 256, :])[:, :, off:off + n],
                                          in_=ob[i][:, :, 0:n]), reads=[r_ob[i]])
        return None

    v_epi = make_v_epi(C, C.vtok, 4096)
    supers = [([(W, sc * 512, 512)], [[0, 128], [256, 384]]) for sc in range(8)]
    gemm(C, C.aT, KD, HALF_GROUPS,
         [dict(kind="B", supers=supers, epi=qk_epi),
          dict(kind="A", cols=[(W, 4096 + 512 * i) for i in range(8)], epi=v_epi)],
         list(range(8)), 512)

    phase_begin(C)
    dl = A.alloc([16], F32)
    lg = A.alloc([16], F32)
    relm = A.alloc([2, 128], F32)
    m01 = A.alloc([2, 128], F32)
    qexp = A.alloc([2, 128], F32)
    kexp = A.alloc([2], F32)
    gn = A.alloc([2, 32], F32)
    ident_b = A.alloc([128], BF16)
    kdec = A.alloc([16], F32)
    cdec = A.alloc([16], F32)
    r_p = Res()
    P.dma("sp", lambda e: e.dma_start(out=dl, in_=C.rt_decay.partition_broadcast(128)), writes=[r_p])
    P.dma("sp", lambda e: e.dma_start(out=relm, in_=C.rt_rel), writes=[r_p])
    P.dma("sp", lambda e: e.dma_start(out=m01, in_=C.rt_m01), writes=[r_p])
    P.dma("sp", lambda e: e.dma_start(out=qexp, in_=C.rt_qexp), writes=[r_p])
    P.dma("sp", lambda e: e.dma_start(out=kexp, in_=C.rt_kexp), writes=[r_p])
    P.dma("sp", lambda e: e.dma_start(out=gn, in_=C.rt_gn), writes=[r_p])
    P.dma("pool", lambda e: e.dma_start(out=ident_b, in_=C.ident_d), writes=[r_p])
    P.op("act", lambda e: e.activation(dl, dl, AF.Exp, scale=-1.0), reads=[r_p], writes=[r_p])
    P.op("act", lambda e: e.activation(lg, dl, AF.Ln, bias=C.one_t[:, 0:1]), reads=[r_p, C.r_const], writes=[r_p])
    P.op("dve", lambda e: e.tensor_scalar(out=lg, in0=lg, scalar1=-1.0, scalar2=None, op0=ALU.mult),
         reads=[r_p], writes=[r_p])
    for dh in range(16):
        d_ = dh // 8
        P.op("act", lambda e, dh=dh, d_=d_: e.activation(kdec[:, dh:dh + 1], kexp[:, d_:d_ + 1], AF.Exp,
                                                         scale=lg[:, dh:dh + 1]), reads=[r_p], writes=[r_p])
    P.op("act", lambda e: e.activation(cdec, lg, AF.Exp, scale=128.0), reads=[r_p], writes=[r_p])

    qT = A.alloc([2, T], BF16)
    kT = A.alloc([2, T], BF16)
    Vsb = A.alloc([34, 512], BF16)
    kd = [A.alloc([34, 256], BF16) for _ in range(2)]
    r_q, r_k, r_v, r_kd = Res(), Res(), Res(), Res()
    maskt = A.alloc([2, 128], F32)
    qdect = A.alloc([2, 128], F32)
    r_hd = Res()
    S = [A.alloc([2, 512], F32) for _ in range(2)]
    Sbf = [[A.alloc([2, 512], BF16) for _ in range(2)] for _ in range(2)]
    r_S = [Res(), Res()]
    r_Sbf = [[Res(), Res()], [Res(), Res()]]
    oacc = [[A.alloc([4, 512], F32) for _ in range(2)] for _ in range(2)]
    r_oacc = [[Res(), Res()], [Res(), Res()]]
    obf2 = [A.alloc([4, 256], BF16) for _ in range(2)]
    osq2 = [A.alloc([4, 256], BF16) for _ in range(2)]
    r_obf2 = [Res(), Res()]
    mean = A.alloc([256], F32)
    msq = A.alloc([256], F32)
    var = A.alloc([256], F32)
    rs = A.alloc([256], F32)
    rstd = A.alloc([256], F32)
    r_st = Res()
    tt = [A.alloc([256], F32) for _ in range(2)]
    r_tt = [Res(), Res()]
    yb = [A.alloc([4, 256], BF16) for _ in range(2)]
    r_yb = [Res(), Res()]
    pmb = [A.alloc([128], BF16) for _ in range(2)]
    r_pm = [Res(), Res()]
    qd = [A.alloc([2, 128], BF16) for _ in range(2)]
    r_qd = [Res(), Res()]
    r_in = [Res() for _ in range(4)]
    psb = C.ps.rearrange("p b f -> p (b f)").bitcast(BF16).rearrange("p (b f) -> p b f", b=8)
    orders = [list(range(34)), [1, 0] + list(range(33, 1, -1))]
    slot_i = [0]
    yb_i = [0]
    tt_i = [0]

    for h in range(8):
        P.dma("sp", lambda e, h=h: e.dma_start(out=qT, in_=kview(C.qkT[h * 256:(h + 1) * 256, :])), writes=[r_q])
        P.dma("sp", lambda e, h=h: e.dma_start(out=kT, in_=kview(C.qkT[(8 + h) * 256:(9 + h) * 256, :])),
              writes=[r_k])
        P.dma("sp", lambda e, h=h: e.dma_start(
            out=Vsb, in_=C.vtok[:, h * 512:(h + 1) * 512].rearrange("(c p) d -> p c d", p=128)), writes=[r_v])
        for d_ in range(2):
            dh = d_ * 8 + h
            P.op("act", lambda e, d_=d_, dh=dh: e.activation(maskt[:, d_, :], relm[:, d_, :], AF.Exp,
                                                             scale=lg[:, dh:dh + 1]), reads=[r_p], writes=[r_hd])
            P.op("dve", lambda e, d_=d_: e.tensor_tensor(out=maskt[:, d_, :], in0=maskt[:, d_, :], in1=m01[:, d_, :],
                                                         op=ALU.mult), reads=[r_p], writes=[r_hd])
            P.op("act", lambda e, d_=d_, dh=dh: e.activation(qdect[:, d_, :], qexp[:, d_, :], AF.Exp,
                                                             scale=lg[:, dh:dh + 1]), reads=[r_p], writes=[r_hd])
            P.op("pool", lambda e, d_=d_: e.memset(S[d_], 0.0), writes=[r_S[d_]])
            P.op("pool", lambda e, d_=d_: e.memset(Sbf[d_][0], 0.0), writes=[r_Sbf[d_][0]])
        for cg in range(17):
            def tr(e, cg=cg):
                for cc in range(2):
                    for dc in range(2):
                        c = 2 * cg + cc
                        i = e.transpose(psb[:, 7, (cc * 2 + dc) * 128:(cc * 2 + dc + 1) * 128],
                                        kT[:, dc, c * 128:(c + 1) * 128], ident_b)
                return i
            P.op("pe", tr, reads=[r_k, r_p], writes=[C.rps[7]])
            src = psb[:, 7, 0:512].rearrange("p (c f) -> p c f", c=2)
            P.op("act", lambda e, cg=cg, src=src, h=h: e.activation(kd[0][:, 2 * cg:2 * cg + 2, :], src, AF.Identity,
                                                                    scale=kdec[:, h:h + 1]),
                 reads=[C.rps[7], r_p], writes=[r_kd])
            P.op("dve", lambda e, cg=cg, src=src, h=h: e.tensor_scalar(out=kd[1][:, 2 * cg:2 * cg + 2, :], in0=src,
                                                                       scalar1=kdec[:, 8 + h:9 + h], scalar2=None,
                                                                       op0=ALU.mult),
                 reads=[C.rps[7], r_p], writes=[r_kd])

        pending = []
        tile_cnt = {}
        obuf = [0, 0]

        def group_norm(d_, tile_i, buf, h=h):
            off, n = TILES[tile_i]
            for hs in range(0, n, 256):
                oa = oacc[d_][buf]
                obf, osq, r_obf = obf2[hs // 256], osq2[hs // 256], r_obf2[hs // 256]
                P.op("act", lambda e, oa=oa, hs=hs, obf=obf: e.activation(obf, oa[:, :, hs:hs + 256], AF.Identity),
                     reads=[r_oacc[d_][buf]], writes=[r_obf])
                P.op("act", lambda e, oa=oa, hs=hs, osq=osq: e.activation(osq, oa[:, :, hs:hs + 256], AF.Square),
                     reads=[r_oacc[d_][buf]], writes=[r_obf])

                def later(oa=oa, hs=hs, off=off, obf=obf, osq=osq, r_obf=r_obf):
                    def mm(e):
                        for ec in range(4):
                            e.matmul(C.ps[:, 7, 0:256], C.ones_b, obf[:, ec, :], start=(ec == 0), stop=(ec == 3))
                        for ec in range(4):
                            i = e.matmul(C.ps[:, 7, 256:512], C.ones_b, osq[:, ec, :], start=(ec == 0), stop=(ec == 3))
                        return i
                    P.op("pe", mm, reads=[r_obf, C.r_const], writes=[C.rps[7]])
                    P.op("dve", lambda e: e.tensor_scalar(out=mean, in0=C.ps[:, 7, 0:256], scalar1=1.0 / 512,
                                                          scalar2=None, op0=ALU.mult),
                         reads=[C.rps[7]], writes=[r_st])
                    P.op("dve", lambda e: e.tensor_tensor(out=msq, in0=mean, in1=mean, op=ALU.mult),
                         reads=[r_st], writes=[r_st])
                    P.op("dve", lambda e: e.scalar_tensor_tensor(out=var, in0=C.ps[:, 7, 256:512], scalar=1.0 / 512,
                                                                 in1=msq, op0=ALU.mult, op1=ALU.subtract),
                         reads=[C.rps[7], r_st], writes=[r_st])
                    P.op("dve", lambda e: e.tensor_scalar(out=var, in0=var, scalar1=0.0, scalar2=None, op0=ALU.max),
                         reads=[r_st], writes=[r_st])
                    P.op("act", lambda e: e.activation(rs, var, AF.Ln, bias=C.eps_t[:, 0:1]),
                         reads=[r_st, C.r_const], writes=[r_st])
                    P.op("act", lambda e: e.activation(rstd, rs, AF.Exp, scale=-0.5), reads=[r_st], writes=[r_st])
                    yi = yb_i[0] % 2
                    yb_i[0] += 1
                    for ec in range(4):
                        ti_ = tt_i[0] % 2
                        tt_i[0] += 1
                        P.op("dve", lambda e, ec=ec, ti_=ti_: e.tensor_tensor(out=tt[ti_], in0=oa[:, ec, hs:hs + 256],
                                                                              in1=mean, op=ALU.subtract),
                             reads=[r_oacc[d_][buf], r_st], writes=[r_tt[ti_]])
                        P.op("dve", lambda e, ti_=ti_: e.tensor_tensor(out=tt[ti_], in0=tt[ti_], in1=rstd, op=ALU.mult),
                             reads=[r_st], writes=[r_tt[ti_]])
                        P.op("act", lambda e, ec=ec, ti_=ti_, yi=yi: e.activation(
                            yb[yi][:, ec, :], tt[ti_], AF.Identity, scale=gn[:, d_, h * 4 + ec:h * 4 + ec + 1]),
                            reads=[r_tt[ti_], r_p], writes=[r_yb[yi]])
                    P.dma("sp", lambda e, yi=yi: e.dma_start(
                        out=kview(C.yT[d_][h * 512:(h + 1) * 512, :])[:, :, off + hs:off + hs + 256], in_=yb[yi]),
                        reads=[r_yb[yi]])
                pending.append(later)

        for s_ in range(34):
            for d_ in range(2):
                dh = d_ * 8 + h
                c = orders[d_][s_]
                r = (2 * s_ + d_) % 4
                sl = slot_i[0] % 2
                slot_i[0] += 1
                cur, nxt = s_ % 2, (s_ + 1) % 2
                ps_in = C.ps[:, 0, r * 128:(r + 1) * 128]

                def mm_in(e, c=c, ps_in=ps_in):
                    e.matmul(ps_in, kT[:, 0, c * 128:(c + 1) * 128], qT[:, 0, c * 128:(c + 1) * 128],
                             start=True, stop=False)
                    return e.matmul(ps_in, kT[:, 1, c * 128:(c + 1) * 128], qT[:, 1, c * 128:(c + 1) * 128],
                                    start=False, stop=True)
                P.op("pe", mm_in, reads=[r_k, r_q], writes=[r_in[r]])
                P.op("dve", lambda e, sl=sl, ps_in=ps_in, d_=d_: e.tensor_tensor(out=pmb[sl], in0=ps_in,
                                                                                 in1=maskt[:, d_, :], op=ALU.mult),
                     reads=[r_in[r], r_hd], writes=[r_pm[sl]])
                for dc in range(2):
                    P.op("pool", lambda e, sl=sl, dc=dc, c=c, d_=d_: e.tensor_tensor(
                        out=qd[sl][:, dc, :], in0=qT[:, dc, c * 128:(c + 1) * 128], in1=qdect[:, d_, :], op=ALU.mult),
                        reads=[r_q, r_hd], writes=[r_qd[sl]])
                bo = 1 + d_
                bu = 3 + 2 * d_

                def mm_u(e, c=c, d_=d_, bu=bu):
                    e.matmul(C.ps[:, bu, :], kd[d_][:, c, 0:128], Vsb[:, c, :], start=True, stop=True)
                    return e.matmul(C.ps[:, bu + 1, :], kd[d_][:, c, 128:256], Vsb[:, c, :], start=True, stop=True)
                P.op("pe", mm_u, reads=[r_kd, r_v], writes=[C.rps[bu], C.rps[bu + 1]])

                def mm_o(e, c=c, d_=d_, bo=bo, sl=sl, cur=cur):
                    for ec in range(4):
                        o_ = C.ps[:, bo, ec * 128:(ec + 1) * 128]
                        e.matmul(o_, Sbf[d_][cur][:, 0, ec * 128:(ec + 1) * 128], qd[sl][:, 0, :], start=True, stop=False)
                        e.matmul(o_, Sbf[d_][cur][:, 1, ec * 128:(ec + 1) * 128], qd[sl][:, 1, :], start=False, stop=False)
                        i = e.matmul(o_, Vsb[:, c, ec * 128:(ec + 1) * 128], pmb[sl], start=False, stop=True)
                    return i
                P.op("pe", mm_o, reads=[r_Sbf[d_][cur], r_qd[sl], r_pm[sl], r_v], writes=[C.rps[bo]])
                todo, pending[:] = list(pending), []
                for fn in todo:
                    fn()
                P.op("dve", lambda e, d_=d_, dh=dh, bu=bu: e.scalar_tensor_tensor(
                    out=S[d_], in0=S[d_], scalar=cdec[:, dh:dh + 1], in1=C.ps[:, bu:bu + 2, :], op0=ALU.mult,
                    op1=ALU.add), reads=[C.rps[bu], C.rps[bu + 1], r_p], writes=[r_S[d_]])
                P.op("act", lambda e, d_=d_, nxt=nxt: e.activation(Sbf[d_][nxt], S[d_], AF.Identity),
                     reads=[r_S[d_]], writes=[r_Sbf[d_][nxt]])
                if c < 2:
                    tile_i, pos, need = 0, c, 2
                else:
                    tile_i, pos, need = 1 + (c - 2) // 4, (c - 2) % 4, 4
                buf = obuf[d_]
                src_o = C.ps[:, bo, :].rearrange("p (a q) -> p a q", a=4)
                P.op("pool" if False else "act", lambda e, d_=d_, buf=buf, pos=pos, src_o=src_o: e.activation(
                    oacc[d_][buf][:, :, pos * 128:(pos + 1) * 128], src_o, AF.Identity),
                    reads=[C.rps[bo]], writes=[r_oacc[d_][buf]])
                key = (d_, tile_i)
                tile_cnt[key] = tile_cnt.get(key, 0) + 1
                if tile_cnt[key] == need:
                    group_norm(d_, tile_i, buf)
                    obuf[d_] = 1 - buf
        for fn in pending:
            fn()
        pending[:] = []

    phase_begin(C)
    NB3 = 3
    yf_t = [A.alloc([512], BF16) for _ in range(NB3)]
    yb_t = [A.alloc([512], BF16) for _ in range(NB3)]
    sf = [A.alloc([512], F32) for _ in range(2)]
    sb_ = [A.alloc([512], F32) for _ in range(2)]
    yc = [A.alloc([512], BF16) for _ in range(2)]
    r_y3 = [Res() for _ in range(NB3)]
    r_sf, r_sb, r_yc = [Res(), Res()], [Res(), Res()], [Res(), Res()]
    gcnt = [0]

    def g_epi(si, ui, tile, banks, ti, nt):
        off, n = tile
        j = si * 2 + ui
        bf_, bb_ = banks
        i3 = gcnt[0] % NB3
        i2 = gcnt[0] % 2
        gcnt[0] += 1
        P.dma("sp", lambda e: e.dma_start(out=yf_t[i3][:, 0:n], in_=C.yT[0][j * 128:(j + 1) * 128, off:off + n]),
              writes=[r_y3[i3]])
        P.dma("sp", lambda e: e.dma_start(out=yb_t[i3][:, 0:n], in_=C.yT[1][j * 128:(j + 1) * 128, off:off + n]),
              writes=[r_y3[i3]])
        P.op("act", lambda e: e.activation(sf[i2][:, 0:n], C.ps[:, bf_, 0:n], AF.Silu),
             reads=[C.rps[bf_]], writes=[r_sf[i2]])
        P.op("act", lambda e: e.activation(sb_[i2][:, 0:n], C.ps[:, bb_, 0:n], AF.Silu),
             reads=[C.rps[bb_]], writes=[r_sb[i2]])
        P.op("dve", lambda e: e.tensor_tensor(out=sf[i2][:, 0:n], in0=sf[i2][:, 0:n], in1=yf_t[i3][:, 0:n],
                                              op=ALU.mult), reads=[r_y3[i3]], writes=[r_sf[i2]])
        P.op("dve", lambda e: e.tensor_tensor(out=sb_[i2][:, 0:n], in0=sb_[i2][:, 0:n], in1=yb_t[i3][:, 0:n],
                                              op=ALU.mult), reads=[r_y3[i3]], writes=[r_sb[i2]])
        P.op("pool", lambda e: e.tensor_tensor(out=yc[i2][:, 0:n], in0=sf[i2][:, 0:n], in1=sb_[i2][:, 0:n],
                                               op=ALU.add), reads=[r_sf[i2], r_sb[i2]], writes=[r_yc[i2]])
        P.dma("sp", lambda e: e.dma_start(out=C.oT[j * 128:(j + 1) * 128, off:off + n], in_=yc[i2][:, 0:n]),
              reads=[r_yc[i2]])
        return None

    supers = [([(W, 8192 + sc * 256, 256), (W, 12288 + sc * 256, 256)], [[0, 256], [128, 384]]) for sc in range(16)]
    gemm(C, C.aT, KD, HALF_GROUPS, [dict(kind="B", supers=supers, epi=g_epi)], list(range(8)), 512)

    phase_begin(C)
    Wo = C.rt_wo[0]
    epi = make_resid_epi(C, lambda c, col: C.mod[:, 32 + c, col:col + 1], lambda si, ui: si * 2 + ui)
    supers = [([(Wo, sc * 256, 256)], [[0], [128]]) for sc in range(8)]
    gemm(C, C.oT, 32, HALF_GROUPS, [dict(kind="B", supers=supers, epi=epi)], list(range(8)), 256)


def build_program(nl=NL):
    nc = bass.Bass("TRN2", target_bir_lowering=False)
    C = Ctx()
    C.nc = nc
    C.P = Prog(nc)

    def din(name, shape, dt=F32):
        return nc.dram_tensor(name, list(shape), dt, kind="ExternalInput").ap()

    C.x_in = din("x", [NLAT, D])
    C.ctx_in = din("ctx", [NCTX, D])
    C.cvec = din("cvec", [128, KD, 2])
    C.modb_d = din("modb", [128, 4, 96])
    C.normg_d = din("normg", [128, 4, 2, 16])
    C.convw = din("convw", [128, 4, 88, 3])
    C.convb = din("convb", [128, 4, 88])
    C.mod_w = din("mod_w", [4, D, 6 * D])
    C.ffn_w_in = din("ffn_w_in", [4, D, 2 * DFF])
    C.ffn_w_out = din("ffn_w_out", [4, DFF, D])
    C.ga_wqkv = din("ga_wqkv", [2, D, 3072])
    C.ga_wo = din("ga_wo", [2, D, D])
    C.ga_qkg = din("ga_qkg", [128, 2, 2])
    C.ga_sink = din("ga_sink", [2, 16])
    C.df_wqkv = din("df_wqkv", [1, D, 6144])
    C.df_wo = din("df_wo", [1, D, D])
    C.df_qkg = din("df_qkg", [128, 2])
    C.df_subln = din("df_subln", [128, 2])
    C.df_lam = din("df_lam", [4, 128])
    C.rt_w_in = din("rt_w_in", [1, D, 16384])
    C.rt_wo = din("rt_wo", [1, 4096, D])
    C.rt_decay = din("rt_decay", [16])
    C.rt_gn = din("rt_gn", [128, 2, 32])
    C.rt_cos = din("rt_cos", [128, NLAT])
    C.rt_sin = din("rt_sin", [128, NLAT])
    C.rt_rel = din("rt_rel", [128, 2, 128])
    C.rt_m01 = din("rt_m01", [128, 2, 128])
    C.rt_qexp = din("rt_qexp", [128, 2, 128])
    C.rt_kexp = din("rt_kexp", [128, 2])
    C.ident_d = din("ident", [128, 128])
    C.rotm_d = din("rotm", [128, 128])
    C.cos_ax = din("cos_ax", [128, NLAT])
    C.sin_ax = din("sin_ax", [128, NLAT])
    C.mask_prev = din("mask_prev", [128, 4, 128])
    C.mask_next = din("mask_next", [128, 4, 128])
    C.out = nc.dram_tensor("out", [NLAT, D], F32, kind="ExternalOutput").ap()
    def scratch(name, shape, dt):
        if name in DEBUG_OUT:
            return nc.dram_tensor(name, shape, dt, kind="ExternalOutput").ap()
        return nc.dram_tensor(name, shape, dt).ap()
    C.hT = scratch("hT", [D, T], F32)
    C.aT = scratch("aT", [D, T], BF16)
    C.gT = scratch("gT", [DFF, T], BF16)
    C.qkT = scratch("qkT", [32 * 128, T], BF16)
    C.vtok = scratch("vtok", [T, 4096], BF16)
    C.oT = scratch("oT", [4096, T], BF16)
    C.oF = scratch("oF", [D, T], F32)
    C.yT = scratch("yT", [2, 4096, T], BF16)

    C.rt_as_df = RT_AS_DF
    C.A = Arena(nc, 206 * 1024)
    A, P = C.A, C.P
    C.ps = nc.alloc_psum_tensor("ps", [128, 8, 512], F32).ap()
    C.rps = [Res() for _ in range(8)]
    C.r_const = Res()
    C.r_mod = Res()
    C.r_lay = Res()
    C.ident_f = A.alloc([128], F32)
    C.ones_b = A.alloc([128], BF16)
    C.rotm_b = A.alloc([128], BF16)
    C.eps_t = A.alloc([1], F32)
    C.one_t = A.alloc([1], F32)
    C.mod = A.alloc([96, 2], F32)
    C.gs = A.alloc([2, 16, 2], F32)
    C.sc_b = A.alloc([KD, 2], BF16)
    C.modb = A.alloc([4, 96], F32)
    C.normg = A.alloc([4, 2, 16], F32)
    cv = A.alloc([KD, 2], F32)
    C.persist_off = A.off
    P.dma("sp", lambda e: e.dma_start(out=C.ident_f, in_=C.ident_d), writes=[C.r_const])
    P.dma("pool", lambda e: e.dma_start(out=C.rotm_b, in_=C.rotm_d), writes=[C.r_const])
    P.dma("sp", lambda e: e.dma_start(out=C.modb, in_=C.modb_d), writes=[C.r_const])
    P.dma("sp", lambda e: e.dma_start(out=C.normg, in_=C.normg_d), writes=[C.r_const])
    P.dma("sp", lambda e: e.dma_start(out=cv, in_=C.cvec), writes=[C.r_const])
    P.op("dve", lambda e: e.memset(C.ones_b, 1.0), writes=[C.r_const])
    P.op("dve", lambda e: e.memset(C.eps_t, EPS), writes=[C.r_const])
    P.op("dve", lambda e: e.memset(C.one_t, 1.0), writes=[C.r_const])
    P.op("act", lambda e: e.activation(C.sc_b, cv, AF.Silu), reads=[C.r_const], writes=[C.r_const])

    phase_input(C)
    for l in range(nl):
        kind, j = l % 3, l // 3
        phase_adaln(C, l)
        phase_norm(C, l, 0)
        need_ctx = l < 3
        if kind == 0:
            phase_ga(C, l, j, with_ctx=need_ctx)
        elif kind == 2:
            phase_df(C, l)
        elif C.rt_as_df:
            phase_df(C, l)
        else:
            phase_rt(C, l)
        if STOP_AFTER_MIXER == l:
            break
        phase_norm(C, l, 1, with_ctx=need_ctx)
        phase_ffn(C, l, with_ctx=need_ctx)
    phase_output(C)
    P.emit()
    return nc


def host_consts():
    ident = np.eye(128, dtype=np.float32)
    rotm = np.zeros((128, 128), np.float32)
    for m in range(64):
        rotm[m + 64, m] = -1.0
        rotm[m, m + 64] = 1.0
    rows = np.repeat(np.arange(64), 64).astype(np.float32)
    cols = np.tile(np.arange(64), 64).astype(np.float32)
    nf = 32
    inv = (np.float32(10000.0) ** (-np.arange(nf, dtype=np.float32) / nf)).astype(np.float32)
    ang = np.concatenate([rows[:, None] * inv, cols[:, None] * inv], -1).astype(np.float32)
    cos, sin = np.cos(ang).astype(np.float32), np.sin(ang).astype(np.float32)
    cos_ax = np.ascontiguousarray(np.concatenate([cos, cos], -1).T)
    sin_ax = np.ascontiguousarray(np.concatenate([sin, sin], -1).T)
    jj = np.arange(128)[:, None]
    ii = np.arange(128)[None, :]
    mp = (jj >= ii).astype(np.float32)
    mn = (jj <= ii).astype(np.float32)
    mask_prev = np.ascontiguousarray(np.repeat(mp[:, None, :], 4, 1))
    mask_next = np.ascontiguousarray(np.repeat(mn[:, None, :], 4, 1))
    invr = (np.float32(10000.0) ** (-np.arange(128, dtype=np.float32) / 128)).astype(np.float32)
    angr = (np.arange(NLAT, dtype=np.float32)[:, None] * invr).astype(np.float32)
    rt_cos = np.ascontiguousarray(np.cos(angr).astype(np.float32).T)
    rt_sin = np.ascontiguousarray(np.sin(angr).astype(np.float32).T)
    jf = jj.astype(np.float32)
    if_ = ii.astype(np.float32)
    rel = np.stack([np.maximum(if_ - jf, 0.0), np.maximum(jf - if_, 0.0)], 1).astype(np.float32)
    m01 = np.stack([(ii >= jj), (jj >= ii)], 1).astype(np.float32)
    pos = np.arange(128, dtype=np.float32)
    qexp = np.ascontiguousarray(np.broadcast_to(np.stack([pos + 1.0, 128.0 - pos], 0)[None], (128, 2, 128))).astype(np.float32)
    kexp = np.stack([127.0 - pos, pos], 1).astype(np.float32)
    return dict(ident=ident, rotm=rotm, cos_ax=cos_ax, sin_ax=sin_ax, mask_prev=mask_prev, mask_next=mask_next,
                rt_cos=rt_cos, rt_sin=rt_sin, rt_rel=np.ascontiguousarray(rel), rt_m01=np.ascontiguousarray(m01),
                rt_qexp=qexp, rt_kexp=np.ascontiguousarray(kexp))


def fm(v, lead=()):
    v = np.asarray(v, np.float32)
    n = v.shape[-1] // 128
    w = v.reshape(v.shape[:-1] + (n, 128))
    return np.ascontiguousarray(np.moveaxis(w, -1, 0))


def make_in_maps(inputs, ncores=NCORES):
    c = host_consts()
    shared = dict(c)
    shared["modb"] = fm(inputs["mod_b"])
    shared["normg"] = fm(inputs["norm_g"])
    shared["convw"] = np.ascontiguousarray(np.transpose(fm(inputs["ffn_conv_w"]), (0, 1, 3, 2)))
    shared["convb"] = fm(inputs["ffn_conv_b"])
    shared["ga_qkg"] = np.ascontiguousarray(fm(inputs["ga_qk_norm"])[:, :, :, 0])
    shared["ga_sink"] = np.ascontiguousarray(inputs["ga_sink"], np.float32)
    shared["df_qkg"] = np.ascontiguousarray(fm(inputs["df_qk_norm"])[:, 0, :, 0])
    shared["df_subln"] = np.ascontiguousarray(fm(inputs["df_subln"])[:, 0, :])
    shared["df_lam"] = np.ascontiguousarray(inputs["df_lambda"][0], np.float32)
    shared["rt_decay"] = np.ascontiguousarray(inputs["rt_decay"][0].reshape(16), np.float32)
    shared["rt_gn"] = np.ascontiguousarray(fm(inputs["rt_gn"])[:, 0, :, :])
    for k in ("mod_w", "ffn_w_in", "ffn_w_out", "ga_wqkv", "ga_wo", "df_wqkv", "df_wo", "rt_w_in", "rt_wo"):
        shared[k] = np.ascontiguousarray(inputs[k], np.float32)
    maps = []
    for b in range(ncores):
        m = dict(shared)
        m["x"] = np.ascontiguousarray(inputs["x"][b], np.float32)
        m["ctx"] = np.ascontiguousarray(inputs["ctx"][b], np.float32)
        cv = np.stack([fm(inputs["c"][b]), fm(inputs["c_ctx"])], -1)
        m["cvec"] = np.ascontiguousarray(cv)
        maps.append(m)
    return maps


def kernel(**inputs):
    nc = build_program(NL)
    maps = make_in_maps(inputs)
    maps = maps[:NCORES]
    res = run_bass_kernel_spmd(nc, maps, core_ids=list(range(NCORES)))
    if DEBUG_OUT:
        global LAST_RESULTS
        LAST_RESULTS = res.results
    out = np.stack([np.asarray(r["out"], np.float32) for r in res.results], 0)
    return out
```

```python
import math
import numpy as np
import concourse.bass as bass
import concourse.mybir as mybir
from concourse.bass_utils import run_bass_kernel_spmd

F32 = mybir.dt.float32
BF16 = mybir.dt.bfloat16
AF = mybir.ActivationFunctionType
ALU = mybir.AluOpType

D = 2048
KD = 16
NCTX = 256
NLAT = 4096
T = NCTX + NLAT
DFF = 5632
EPS = 1e-6
TILES = [(0, 256)] + [(256 + 512 * i, 512) for i in range(8)]
ENGS = ["pe", "act", "dve", "pool", "sp"]
NL = 4
NCORES = 4
DEBUG_OUT = []
STOP_AFTER_MIXER = None
RT_AS_DF = False


class Res:
    __slots__ = ("w", "r")

    def __init__(self):
        self.w = None
        self.r = []


class Prog:
    def __init__(self, nc, n_dma_sems=10):
        self.nc = nc
        self.ops = {e: [] for e in ENGS}
        self.known = {e: {} for e in ENGS}
        self.n_dma_sems = n_dma_sems
        self.dma_rr = {"sp": 0, "pool": 0}
        self.semkeys = list(ENGS)
        for q in ("sp", "pool"):
            for i in range(n_dma_sems):
                self.semkeys.append(f"d_{q}_{i}")
        self.cnt = {k: 0 for k in self.semkeys}

    def _deps(self, reads, writes):
        evs = []
        for r in reads:
            if r.w is not None:
                evs.append(r.w)
        for w in writes:
            if w.w is not None:
                evs.append(w.w)
            evs.extend(w.r)
        return evs

    def _commit(self, ev, reads, writes):
        for r in reads:
            r.r.append(ev)
        for w in writes:
            w.w = ev
            w.r = []

    def _waits(self, eng, evs):
        kn = self.known[eng]
        need = {}
        for (k, v) in evs:
            if k == eng and eng == "pe":
                continue
            if kn.get(k, 0) < v and need.get(k, 0) < v:
                need[k] = v
        for k, v in need.items():
            kn[k] = v
        return list(need.items())

    def op(self, eng, fn, reads=(), writes=()):
        evs = self._deps(reads, writes)
        waits = self._waits(eng, evs)
        self.cnt[eng] += 1
        ev = (eng, self.cnt[eng])
        self.ops[eng].append((waits, fn, (eng, 1)))
        self._commit(ev, reads, writes)
        return ev

    def dma(self, q, fn, reads=(), writes=()):
        evs = self._deps(reads, writes)
        i = self.dma_rr[q]
        self.dma_rr[q] = (i + 1) % self.n_dma_sems
        k = f"d_{q}_{i}"
        if self.cnt[k] > 0:
            evs.append((k, self.cnt[k]))
        waits = self._waits(q, evs)
        self.cnt[k] += 16
        ev = (k, self.cnt[k])
        self.ops[q].append((waits, fn, (k, 16)))
        self._commit(ev, reads, writes)
        return ev

    def barrier(self):
        evs = [(k, v) for k, v in self.cnt.items() if v > 0]
        for e in ENGS:
            waits = self._waits(e, evs)
            if waits:
                self.ops[e].append((waits, None, None))

    def emit(self):
        from contextlib import ExitStack
        nc = self.nc
        sems = {}
        with ExitStack() as es:
            for k in self.semkeys:
                sems[k] = es.enter_context(nc.semaphore("s_" + k))
            block = es.enter_context(nc.Block())
            final = [(k, v) for k, v in self.cnt.items() if v > 0]

            def run(engname, e):
                for waits, fn, inc in self.ops[engname]:
                    for (k, v) in waits:
                        e.wait_ge(sems[k], v)
                    if fn is not None:
                        inst = fn(e)
                        inst.then_inc(sems[inc[0]], inc[1])

            @block.tensor
            def _(e):
                run("pe", e)

            @block.scalar
            def _(e):
                run("act", e)

            @block.vector
            def _(e):
                run("dve", e)

            @block.gpsimd
            def _(e):
                run("pool", e)

            @block.sync
            def _(e):
                run("sp", e)
                for (k, v) in final:
                    e.wait_ge(sems[k], v)


class Arena:
    def __init__(self, nc, nbytes):
        self.ap = nc.alloc_sbuf_tensor("arena", [128, nbytes // 2], BF16).ap()
        self.cap = nbytes // 2
        self.off = 0

    def alloc(self, shape, dt):
        n = 1
        for s in shape:
            n *= s
        n16 = n * (2 if dt == F32 else 1)
        n16 = (n16 + 15) // 16 * 16
        assert self.off + n16 <= self.cap, f"SBUF arena overflow {(self.off + n16) * 2}"
        v = self.ap[:, self.off:self.off + (n * (2 if dt == F32 else 1))]
        self.off += n16
        if dt == F32:
            v = v.bitcast(F32)
        if len(shape) == 2:
            v = v.rearrange("p (a b) -> p a b", a=shape[0])
        elif len(shape) == 3:
            v = v.rearrange("p (a b c) -> p a b c", a=shape[0], b=shape[1])
        elif len(shape) == 4:
            v = v.rearrange("p (a b c d) -> p a b c d", a=shape[0], b=shape[1], c=shape[2])
        return v


class Ctx:
    pass


def kview(ap2d):
    return ap2d.rearrange("(k p) n -> p k n", p=128)


def gemm(C, aT, KC, groups, passes, main_banks, slot_cols):
    P, A = C.P, C.A
    gmax = max(sum(n for _, n in g) for g in groups)
    asb = A.alloc([KC, gmax], BF16)
    slots = [A.alloc([KC, slot_cols], BF16) for _ in range(2)]
    r_slot = [Res(), Res()]
    aTv = kview(aT)
    wi = 0
    bank_pos = [0]
    deferred = [None]

    def next_banks(nb):
        b = [main_banks[(bank_pos[0] + i) % len(main_banks)] for i in range(nb)]
        bank_pos[0] = (bank_pos[0] + nb) % len(main_banks)
        return b

    def run_deferred():
        d = deferred[0]
        deferred[0] = None
        if d is not None:
            d()

    for g in groups:
        r_a = []
        loc = []
        o = 0
        for (off, n) in g:
            r = Res()
            P.dma("sp", lambda e, o=o, off=off, n=n: e.dma_start(out=asb[:, :, o:o + n], in_=aTv[:, :, off:off + n]),
                  writes=[r])
            r_a.append(r)
            loc.append(o)
            o += n
        work = []
        for ps_ in passes:
            if ps_["kind"] == "B":
                for si, (pieces, units) in enumerate(ps_["supers"]):
                    def load(s, pieces=pieces):
                        c = 0
                        for (W2d, c0, w) in pieces:
                            P.dma("pool", lambda e, s=s, c=c, W2d=W2d, c0=c0, w=w: e.dma_start(
                                out=slots[s][:, :, c:c + w], in_=kview(W2d)[:, :, c0:c0 + w]), writes=[r_slot[s]])
                            c += w

                    def comp(s, si=si, units=units, epi=ps_["epi"]):
                        for ui, unit in enumerate(units):
                            for ti, (off, n) in enumerate(g):
                                banks = next_banks(len(unit))

                                def mm(e, s=s, unit=unit, banks=banks, lo=loc[ti], n=n):
                                    for ci, co in enumerate(unit):
                                        for k in range(KC):
                                            i = e.matmul(C.ps[:, banks[ci], 0:n], slots[s][:, k, co:co + 128],
                                                         asb[:, k, lo:lo + n], start=(k == 0), stop=(k == KC - 1))
                                    return i
                                P.op("pe", mm, reads=[r_slot[s], r_a[ti]], writes=[C.rps[b] for b in banks])
                                run_deferred()
                                deferred[0] = epi(si, ui, (off, n), banks, ti, len(g))
                    work.append((load, comp, False))
                work[-1] = (work[-1][0], work[-1][1], True)
            else:
                for fi, (W2d, c0) in enumerate(ps_["cols"]):
                    def load(s, W2d=W2d, c0=c0):
                        P.dma("pool", lambda e, s=s, W2d=W2d, c0=c0: e.dma_start(
                            out=slots[s][:, :, 0:512], in_=kview(W2d)[:, :, c0:c0 + 512]), writes=[r_slot[s]])

                    def comp(s, fi=fi, epi=ps_["epi"]):
                        for ti, (off, n) in enumerate(g):
                            for tc in range(n // 128):
                                banks = next_banks(1)

                                def mm(e, s=s, b=banks[0], lo=loc[ti] + tc * 128):
                                    for k in range(KC):
                                        i = e.matmul(C.ps[:, b, :], asb[:, k, lo:lo + 128], slots[s][:, k, 0:512],
                                                     start=(k == 0), stop=(k == KC - 1))
                                    return i
                                P.op("pe", mm, reads=[r_slot[s], r_a[ti]], writes=[C.rps[banks[0]]])
                                run_deferred()
                                deferred[0] = epi(fi, off + tc * 128, banks[0])
                    work.append((load, comp, False))
                work[-1] = (work[-1][0], work[-1][1], True)
        slot_of = []
        for i in range(len(work)):
            slot_of.append(wi % 2)
            wi += 1
        if work:
            work[0][0](slot_of[0])
        for i, (load, comp, last_of_pass) in enumerate(work):
            if i + 1 < len(work):
                work[i + 1][0](slot_of[i + 1])
            comp(slot_of[i])
            if last_of_pass:
                run_deferred()


def phase_begin(C):
    C.P.barrier()
    C.A.off = C.persist_off
    for r in C.rps:
        r.w = None
        r.r = []


def make_resid_epi(C, gate_of, chunk_of):
    P, A = C.P, C.A
    NB = 3
    hts = [A.alloc([512], F32) for _ in range(NB)]
    hns = [A.alloc([512], F32) for _ in range(NB)]
    r_ht = [Res() for _ in range(NB)]
    r_hn = [Res() for _ in range(NB)]
    cnt = [0]

    def epi(si, ui, tile, banks, ti, nt):
        off, n = tile
        c = chunk_of(si, ui)
        i = cnt[0] % NB
        cnt[0] += 1
        col = 1 if off < NCTX else 0
        src = C.hT[c * 128:(c + 1) * 128, off:off + n]
        P.dma("sp", lambda e: e.dma_start(out=hts[i][:, 0:n], in_=src), writes=[r_ht[i]])
        g = gate_of(c, col)
        b = banks[0]
        P.op("dve", lambda e: e.scalar_tensor_tensor(out=hns[i][:, 0:n], in0=C.ps[:, b, 0:n], scalar=g,
                                                     in1=hts[i][:, 0:n], op0=ALU.mult, op1=ALU.add),
             reads=[C.rps[b], r_ht[i]], writes=[r_hn[i]])
        P.dma("sp", lambda e: e.dma_start(out=src, in_=hns[i][:, 0:n]), reads=[r_hn[i]])
        return None
    return epi


def phase_input(C):
    P, A = C.P, C.A
    phase_begin(C)
    xin = [A.alloc([D], F32) for _ in range(2)]
    hst = [A.alloc([KD, 128], F32) for _ in range(2)]
    r_x = [Res(), Res()]
    r_h = [Res(), Res()]
    hTv = kview(C.hT)
    bi = 0

    def in_load(tc):
        s = tc % 2
        src = C.ctx_in[tc * 128:(tc + 1) * 128, :] if tc < 2 else C.x_in[(tc - 2) * 128:(tc - 1) * 128, :]
        P.dma("sp", lambda e, s=s, src=src: e.dma_start(out=xin[s], in_=src), writes=[r_x[s]])

    in_load(0)
    for tc in range(T // 128):
        s = tc % 2
        if tc + 1 < T // 128:
            in_load(tc + 1)
        for g4 in range(4):
            b = bi % 8
            bi += 1

            def tr(e, s=s, g4=g4, b=b):
                for j in range(4):
                    k = g4 * 4 + j
                    i = e.transpose(C.ps[:, b, j * 128:(j + 1) * 128], xin[s][:, k * 128:(k + 1) * 128], C.ident_f)
                return i
            P.op("pe", tr, reads=[r_x[s], C.r_const], writes=[C.rps[b]])
            eng = "act" if g4 % 2 == 0 else "dve"
            dst = hst[s][:, g4 * 4:(g4 + 1) * 4, :]
            srcp = C.ps[:, b, :].rearrange("p (j t) -> p j t", j=4)
            if eng == "act":
                P.op("act", lambda e, dst=dst, srcp=srcp: e.activation(dst, srcp, AF.Identity),
                     reads=[C.rps[b]], writes=[r_h[s]])
            else:
                P.op("dve", lambda e, dst=dst, srcp=srcp: e.tensor_copy(dst, srcp),
                     reads=[C.rps[b]], writes=[r_h[s]])
        P.dma("sp", lambda e, s=s, tc=tc: e.dma_start(out=hTv[:, :, tc * 128:(tc + 1) * 128], in_=hst[s]),
              reads=[r_h[s]])


def phase_output(C):
    P, A = C.P, C.A
    phase_begin(C)
    hin = [A.alloc([KD, 128], F32) for _ in range(2)]
    ost = [A.alloc([D], F32) for _ in range(2)]
    r_i = [Res(), Res()]
    r_o = [Res(), Res()]
    hTv = kview(C.hT)
    bi = 0
    def out_load(tc):
        s = tc % 2
        t0 = NCTX + tc * 128
        P.dma("sp", lambda e, s=s, t0=t0: e.dma_start(out=hin[s], in_=hTv[:, :, t0:t0 + 128]), writes=[r_i[s]])

    out_load(0)
    for tc in range(NLAT // 128):
        s = tc % 2
        if tc + 1 < NLAT // 128:
            out_load(tc + 1)
        for g4 in range(4):
            b = bi % 8
            bi += 1

            def tr(e, s=s, g4=g4, b=b):
                for j in range(4):
                    i = e.transpose(C.ps[:, b, j * 128:(j + 1) * 128], hin[s][:, g4 * 4 + j, :], C.ident_f)
                return i
            P.op("pe", tr, reads=[r_i[s], C.r_const], writes=[C.rps[b]])
            dst = ost[s][:, g4 * 512:(g4 + 1) * 512]
            if g4 % 2 == 0:
                P.op("act", lambda e, dst=dst, b=b: e.activation(dst, C.ps[:, b, :], AF.Identity),
                     reads=[C.rps[b]], writes=[r_o[s]])
            else:
                P.op("dve", lambda e, dst=dst, b=b: e.tensor_copy(dst, C.ps[:, b, :]),
                     reads=[C.rps[b]], writes=[r_o[s]])
        P.dma("sp", lambda e, s=s, tc=tc: e.dma_start(out=C.out[tc * 128:(tc + 1) * 128, :], in_=ost[s]),
              reads=[r_o[s]])


def phase_adaln(C, l):
    P, A = C.P, C.A
    phase_begin(C)
    slots = [A.alloc([KD, 1024], BF16) for _ in range(2)]
    r_s = [Res(), Res()]
    tmp = A.alloc([16, 2], F32)
    r_tmp = Res()
    Wv = kview(C.mod_w[l])
    for s8 in range(12):
        s = s8 % 2
        P.dma("pool", lambda e, s=s, s8=s8: e.dma_start(out=slots[s], in_=Wv[:, :, s8 * 1024:(s8 + 1) * 1024]),
              writes=[r_s[s]])
        b = s8 % 8

        def mm(e, s=s, b=b):
            for c in range(8):
                for k in range(KD):
                    i = e.matmul(C.ps[:, b, 2 * c:2 * c + 2], slots[s][:, k, c * 128:(c + 1) * 128], C.sc_b[:, k, :],
                                 start=(k == 0), stop=(k == KD - 1))
            return i
        P.op("pe", mm, reads=[r_s[s], C.r_const], writes=[C.rps[b]])
        for col in range(2):
            P.op("dve", lambda e, b=b, col=col, s8=s8: e.tensor_tensor(
                out=C.mod[:, s8 * 8:(s8 + 1) * 8, col],
                in0=C.ps[:, b, 0:16].rearrange("p (c two) -> p c two", two=2)[:, :, col],
                in1=C.modb[:, l, s8 * 8:(s8 + 1) * 8], op=ALU.add),
                reads=[C.rps[b], C.r_const], writes=[C.r_mod])
    for sub in range(2):
        base = 48 * sub
        P.op("dve", lambda e, base=base: e.tensor_scalar(out=tmp, in0=C.mod[:, base + 16:base + 32, :], scalar1=1.0,
                                                         scalar2=None, op0=ALU.add),
             reads=[C.r_mod], writes=[r_tmp])
        for col in range(2):
            P.op("dve", lambda e, sub=sub, col=col: e.tensor_tensor(out=C.gs[:, sub, :, col], in0=tmp[:, :, col],
                                                                    in1=C.normg[:, l, sub, :], op=ALU.mult),
                 reads=[r_tmp, C.r_const], writes=[C.r_mod])


def phase_norm(C, l, sub, with_ctx=True):
    P, A = C.P, C.A
    phase_begin(C)
    xt = [A.alloc([KD, 512], F32) for _ in range(2)]
    sq = [A.alloc([KD, 512], BF16) for _ in range(2)]
    at = [A.alloc([KD, 512], BF16) for _ in range(2)]
    xn = [A.alloc([512], F32) for _ in range(4)]
    rs = [A.alloc([512], F32) for _ in range(2)]
    rstd = [A.alloc([512], F32) for _ in range(2)]
    r_xt, r_sq, r_at = [Res(), Res()], [Res(), Res()], [Res(), Res()]
    r_xn = [Res() for _ in range(4)]
    r_rs, r_rstd = [Res(), Res()], [Res(), Res()]
    hTv = kview(C.hT)
    aTv = kview(C.aT)
    base = 48 * sub
    xi = 0
    tiles = TILES if with_ctx else TILES[1:]

    def n_load(ti):
        off, n = tiles[ti]
        s = ti % 2
        P.dma("sp", lambda e, s=s, off=off, n=n: e.dma_start(out=xt[s][:, :, 0:n], in_=hTv[:, :, off:off + n]),
              writes=[r_xt[s]])

    n_load(0)
    for ti, (off, n) in enumerate(tiles):
        s = ti % 2
        b = ti % 8
        col = 1 if off < NCTX else 0
        if ti + 1 < len(tiles):
            n_load(ti + 1)
        P.op("act", lambda e, s=s, n=n: e.activation(sq[s][:, :, 0:n], xt[s][:, :, 0:n], AF.Square),
             reads=[r_xt[s]], writes=[r_sq[s]])

        def mm(e, s=s, b=b, n=n):
            for k in range(KD):
                i = e.matmul(C.ps[:, b, 0:n], C.ones_b, sq[s][:, k, 0:n], start=(k == 0), stop=(k == KD - 1))
            return i
        P.op("pe", mm, reads=[r_sq[s], C.r_const], writes=[C.rps[b]])
        P.op("act", lambda e, s=s, b=b, n=n: e.activation(rs[s][:, 0:n], C.ps[:, b, 0:n], AF.Sqrt,
                                                          scale=1.0 / D, bias=C.eps_t[:, 0:1]),
             reads=[C.rps[b], C.r_const], writes=[r_rs[s]])
        P.op("dve", lambda e, s=s, n=n: e.reciprocal(rstd[s][:, 0:n], rs[s][:, 0:n]),
             reads=[r_rs[s]], writes=[r_rstd[s]])
        for k in range(KD):
            x4 = xi % 4
            xi += 1
            P.op("dve", lambda e, s=s, k=k, n=n, x4=x4: e.tensor_tensor(out=xn[x4][:, 0:n], in0=xt[s][:, k, 0:n],
                                                                        in1=rstd[s][:, 0:n], op=ALU.mult),
                 reads=[r_xt[s], r_rstd[s]], writes=[r_xn[x4]])
            P.op("act", lambda e, s=s, k=k, n=n, x4=x4, col=col: e.activation(
                at[s][:, k, 0:n], xn[x4][:, 0:n], AF.Identity,
                scale=C.gs[:, sub, k, col:col + 1], bias=C.mod[:, base + k, col:col + 1]),
                reads=[r_xn[x4], C.r_mod], writes=[r_at[s]])
        P.dma("sp", lambda e, s=s, off=off, n=n: e.dma_start(out=aTv[:, :, off:off + n], in_=at[s][:, :, 0:n]),
              reads=[r_at[s]])


def phase_ffn(C, l, with_ctx=True):
    P, A = C.P, C.A
    phase_begin(C)
    W = C.ffn_w_in[l]
    cw = A.alloc([88, 3], F32)
    cb = A.alloc([88], F32)
    r_cw = Res()
    P.dma("sp", lambda e: e.dma_start(out=cw, in_=C.convw[:, l, :, :]), writes=[r_cw])
    P.dma("sp", lambda e: e.dma_start(out=cb, in_=C.convb[:, l, :]), writes=[r_cw])
    RL = 2432 + 8
    urow = [A.alloc([RL], F32) for _ in range(2)]
    acc = A.alloc([RL], F32)
    accb = A.alloc([RL], F32)
    r_accb = Res()
    grow = [A.alloc([RL], BF16) for _ in range(2)]
    r_u = [Res(), Res()]
    r_acc = Res()
    r_g = [Res(), Res()]
    groups = [
        ([TILES[0], TILES[1], TILES[2], TILES[3], TILES[4], (2304, 128)], 0, 0, 2304),
        ([(2176, 128), TILES[5], TILES[6], TILES[7], TILES[8]], 2176, 2304, 4352),
    ]
    if not with_ctx:
        groups[0] = ([TILES[1], TILES[2], TILES[3], TILES[4], (2304, 128)], NCTX, NCTX, 2304)
    gcount = [0]
    for (g, gstart, own0, own1) in groups:
        for u_ in urow:
            P.op("pool", lambda e, u_=u_: e.memset(u_, 0.0), writes=[r_u[0], r_u[1]])

        def ridx(t, gstart=gstart):
            return (t - gstart) + 1 + (2 if (t >= NCTX and gstart < NCTX) else 0)

        def epi(si, ui, tile, banks, ti, nt, ridx=ridx, own0=own0, own1=own1, g=g):
            off, n = tile
            j = si * 2 + ui
            r0 = ridx(off)
            P.op("act", lambda e: e.activation(urow[0][:, r0:r0 + n], C.ps[:, banks[0], 0:n], AF.Identity),
                 reads=[C.rps[banks[0]]], writes=[r_u[0]])
            P.op("dve", lambda e: e.tensor_copy(urow[1][:, r0:r0 + n], C.ps[:, banks[1], 0:n]),
                 reads=[C.rps[banks[1]]], writes=[r_u[1]])
            if ti != nt - 1:
                return None
            L = ridx(g[-1][0]) + g[-1][1]
            gi = gcount[0] % 2
            gcount[0] += 1

            def conv(eng, dst, src, ch, rdst, rsrc):
                P.op("dve", lambda e: e.tensor_scalar(out=dst[:, 1:L], in0=src[:, 0:L - 1], scalar1=cw[:, ch, 0:1],
                                                    scalar2=cb[:, ch:ch + 1], op0=ALU.mult, op1=ALU.add),
                     reads=[rsrc, r_cw], writes=[rdst])
                P.op("dve", lambda e: e.scalar_tensor_tensor(out=dst[:, 1:L], in0=src[:, 1:L], scalar=cw[:, ch, 1:2],
                                                             in1=dst[:, 1:L], op0=ALU.mult, op1=ALU.add),
                     reads=[rsrc, r_cw], writes=[rdst])
                P.op("dve", lambda e: e.scalar_tensor_tensor(out=dst[:, 1:L], in0=src[:, 2:L + 1], scalar=cw[:, ch, 2:3],
                                                             in1=dst[:, 1:L], op0=ALU.mult, op1=ALU.add),
                     reads=[rsrc, r_cw], writes=[rdst])
            conv("dve", acc, urow[0], j, r_acc, r_u[0])
            conv("dve", accb, urow[1], 44 + j, r_accb, r_u[1])
            P.op("act", lambda e: e.activation(acc[:, 1:L], acc[:, 1:L], AF.Silu), reads=[r_acc], writes=[r_acc])
            P.op("pool", lambda e: e.tensor_tensor(out=grow[gi][:, 1:L], in0=acc[:, 1:L], in1=accb[:, 1:L],
                                                   op=ALU.mult),
                 reads=[r_acc, r_accb], writes=[r_g[gi]])
            segs = []
            if own0 < NCTX:
                segs.append((0, NCTX))
                segs.append((NCTX, own1))
            else:
                segs.append((own0, own1))
            for (a0, a1) in segs:
                ra = ridx(a0)
                P.dma("sp", lambda e, a0=a0, a1=a1, ra=ra: e.dma_start(
                    out=C.gT[j * 128:(j + 1) * 128, a0:a1], in_=grow[gi][:, ra:ra + (a1 - a0)]), reads=[r_g[gi]])
            return None

        supers = []
        for sc in range(22):
            pieces = [(W, sc * 256, 256), (W, DFF + sc * 256, 256)]
            units = [[0, 256], [128, 384]]
            supers.append((pieces, units))
        mark = A.off
        gemm(C, C.aT, KD, [g], [dict(kind="B", supers=supers, epi=epi)], list(range(8)), 512)
        C.P.barrier()
        A.off = mark
    phase_begin(C)
    Wo = C.ffn_w_out[l]
    epi = make_resid_epi(C, lambda c, col: C.mod[:, 80 + c, col:col + 1], lambda si, ui: si * 2 + ui)
    supers = [([(Wo, sc * 256, 256)], [[0], [128]]) for sc in range(8)]
    groups = [[TILES[0], TILES[1], TILES[2]], [TILES[3], TILES[4]], [TILES[5], TILES[6]], [TILES[7], TILES[8]]]
    if not with_ctx:
        groups[0] = [TILES[1], TILES[2]]
    gemm(C, C.gT, 44, groups, [dict(kind="B", supers=supers, epi=epi)], list(range(8)), 256)


def make_qk_epi(C, gvec_of, dst, chunk_of, cosT, sinT, r_tab):
    P, A = C.P, C.A
    NB = 2
    sqb = [A.alloc([512], BF16) for _ in range(NB)]
    gqf = [A.alloc([512], F32) for _ in range(NB)]
    gqb = [A.alloc([512], BF16) for _ in range(NB)]
    rs = [A.alloc([512], F32) for _ in range(NB)]
    rstd = [A.alloc([512], F32) for _ in range(NB)]
    t1 = [A.alloc([512], F32) for _ in range(NB)]
    t2 = [A.alloc([512], F32) for _ in range(NB)]
    ob = [A.alloc([512], BF16) for _ in range(NB)]
    R = lambda: [Res() for _ in range(NB)]
    r_sq, r_gf, r_gb, r_rs, r_rstd, r_t1, r_t2, r_ob = R(), R(), R(), R(), R(), R(), R(), R()
    cnt = [0]

    def epi(si, ui, tile, banks, ti, nt):
        off, n = tile
        ch = chunk_of(si, ui)
        i = cnt[0] % NB
        cnt[0] += 1
        b = banks[0]
        bs = 4 + i
        br = 6 + i
        lat = off >= NCTX
        P.op("act", lambda e: e.activation(sqb[i][:, 0:n], C.ps[:, b, 0:n], AF.Square),
             reads=[C.rps[b]], writes=[r_sq[i]])
        P.op("act", lambda e: e.activation(gqf[i][:, 0:n], C.ps[:, b, 0:n], AF.Identity, scale=gvec_of(ch)),
             reads=[C.rps[b], C.r_lay], writes=[r_gf[i]])
        if lat:
            P.op("pool", lambda e: e.tensor_copy(gqb[i][:, 0:n], gqf[i][:, 0:n]), reads=[r_gf[i]], writes=[r_gb[i]])

        def later():
            P.op("pe", lambda e: e.matmul(C.ps[:, bs, 0:n], C.ones_b, sqb[i][:, 0:n], start=True, stop=True),
                 reads=[r_sq[i], C.r_const], writes=[C.rps[bs]])
            if lat:
                P.op("pe", lambda e: e.matmul(C.ps[:, br, 0:n], C.rotm_b, gqb[i][:, 0:n], start=True, stop=True),
                     reads=[r_gb[i], C.r_const], writes=[C.rps[br]])
            P.op("act", lambda e: e.activation(rs[i][:, 0:n], C.ps[:, bs, 0:n], AF.Ln, scale=1.0 / 128,
                                               bias=C.eps_t[:, 0:1]),
                 reads=[C.rps[bs], C.r_const], writes=[r_rs[i]])
            P.op("act", lambda e: e.activation(rstd[i][:, 0:n], rs[i][:, 0:n], AF.Exp, scale=-0.5),
                 reads=[r_rs[i]], writes=[r_rstd[i]])
            if lat:
                lo = off - NCTX
                P.op("dve", lambda e: e.tensor_tensor(out=t1[i][:, 0:n], in0=gqf[i][:, 0:n], in1=cosT[:, lo:lo + n],
                                                      op=ALU.mult), reads=[r_gf[i], r_tab], writes=[r_t1[i]])
                P.op("dve", lambda e: e.tensor_tensor(out=t2[i][:, 0:n], in0=C.ps[:, br, 0:n], in1=sinT[:, lo:lo + n],
                                                      op=ALU.mult), reads=[C.rps[br], r_tab], writes=[r_t2[i]])
                P.op("pool", lambda e: e.tensor_tensor(out=t1[i][:, 0:n], in0=t1[i][:, 0:n], in1=t2[i][:, 0:n],
                                                       op=ALU.add), reads=[r_t2[i]], writes=[r_t1[i]])
                P.op("dve", lambda e: e.tensor_tensor(out=ob[i][:, 0:n], in0=t1[i][:, 0:n], in1=rstd[i][:, 0:n],
                                                      op=ALU.mult), reads=[r_t1[i], r_rstd[i]], writes=[r_ob[i]])
            else:
                P.op("dve", lambda e: e.tensor_tensor(out=ob[i][:, 0:n], in0=gqf[i][:, 0:n], in1=rstd[i][:, 0:n],
                                                      op=ALU.mult), reads=[r_gf[i], r_rstd[i]], writes=[r_ob[i]])
            P.dma("sp", lambda e: e.dma_start(out=dst[ch * 128:(ch + 1) * 128, off:off + n], in_=ob[i][:, 0:n]),
                  reads=[r_ob[i]])
        return later
    return epi


def make_v_epi(C, dst, ncols):
    P, A = C.P, C.A
    vst = [A.alloc([512], BF16) for _ in range(3)]
    r_v = [Res() for _ in range(3)]
    cnt = [0]

    def epi(fi, tok0, bank):
        i = cnt[0] % 3
        cnt[0] += 1
        if i % 2 == 0:
            P.op("act", lambda e: e.activation(vst[i], C.ps[:, bank, :], AF.Identity),
                 reads=[C.rps[bank]], writes=[r_v[i]])
        else:
            P.op("dve", lambda e: e.tensor_copy(vst[i], C.ps[:, bank, :]), reads=[C.rps[bank]], writes=[r_v[i]])
        P.dma("sp", lambda e: e.dma_start(out=dst[tok0:tok0 + 128, fi * 512:(fi + 1) * 512], in_=vst[i]),
              reads=[r_v[i]])
        return None
    return epi


HALF_GROUPS = [[TILES[0], TILES[1], TILES[2], TILES[3], TILES[4]], [TILES[5], TILES[6], TILES[7], TILES[8]]]


def load_rope_tables(C, cos_d, sin_d):
    P, A = C.P, C.A
    cosT = A.alloc([NLAT], F32)
    sinT = A.alloc([NLAT], F32)
    r_tab = Res()
    P.dma("sp", lambda e: e.dma_start(out=cosT, in_=cos_d), writes=[r_tab])
    P.dma("sp", lambda e: e.dma_start(out=sinT, in_=sin_d), writes=[r_tab])
    return cosT, sinT, r_tab


def phase_ga(C, l, j, with_ctx=True):
    P, A = C.P, C.A
    W = C.ga_wqkv[j]
    phase_begin(C)
    cosT, sinT, r_tab = load_rope_tables(C, C.cos_ax, C.sin_ax)
    gv = A.alloc([2], F32)
    P.dma("sp", lambda e: e.dma_start(out=gv, in_=C.ga_qkg[:, j, :]), writes=[C.r_lay])
    P.op("dve", lambda e: e.tensor_scalar(out=gv[:, 0:1], in0=gv[:, 0:1], scalar1=128 ** -0.5, scalar2=None,
                                          op0=ALU.mult), reads=[C.r_lay], writes=[C.r_lay])
    qk_epi = make_qk_epi(C, lambda ch: gv[:, 0:1] if ch < 16 else gv[:, 1:2], C.qkT,
                         lambda si, ui: si * 4 + ui, cosT, sinT, r_tab)
    v_epi = make_v_epi(C, C.vtok, 512)
    supers = [([(W, sc * 512, 512)], [[0], [128], [256], [384]]) for sc in range(5)]
    gemm(C, C.aT, KD, HALF_GROUPS,
         [dict(kind="B", supers=supers, epi=qk_epi), dict(kind="A", cols=[(W, 2560)], epi=v_epi)],
         [0, 1, 2, 3], 512)
    phase_begin(C)
    sk = A.alloc([16], F32)
    es16 = A.alloc([16], F32)
    zero = A.alloc([128], F32)
    esf = A.alloc([16, 128], F32)
    mprev = A.alloc([4, 128], BF16)
    mnext = A.alloc([4, 128], BF16)
    r_m = Res()
    P.dma("sp", lambda e: e.dma_start(out=sk, in_=C.ga_sink[j].partition_broadcast(128)), writes=[r_m])
    P.dma("pool", lambda e: e.dma_start(out=mprev, in_=C.mask_prev), writes=[r_m])
    P.dma("pool", lambda e: e.dma_start(out=mnext, in_=C.mask_next), writes=[r_m])
    P.op("act", lambda e: e.activation(es16, sk, AF.Exp), reads=[r_m], writes=[r_m])
    P.op("dve", lambda e: e.memset(zero, 0.0), writes=[r_m])
    for h in range(16):
        P.op("dve", lambda e, h=h: e.tensor_scalar(out=esf[:, h, :], in0=zero, scalar1=es16[:, h:h + 1], scalar2=None,
                                                   op0=ALU.add), reads=[r_m], writes=[r_m])
    Ksb = [A.alloc([T], BF16) for _ in range(2)]
    Vsb = [A.alloc([34, 128], BF16) for _ in range(2)]
    Qsb = [A.alloc([4, T], BF16) for _ in range(2)]
    r_kvq = [Res(), Res()]
    NP = 3
    pt = [A.alloc([4, 128], BF16) for _ in range(NP)]
    r_pt = [Res() for _ in range(NP)]
    dn = [A.alloc([4, 128], F32) for _ in range(2)]
    rd = [A.alloc([4, 128], F32) for _ in range(2)]
    r_dn, r_rd = [Res(), Res()], [Res(), Res()]
    ost = [A.alloc([4, 512], BF16) for _ in range(2)]
    r_ost = [Res(), Res()]
    qkv = C.qkT
    oTv = kview(C.oT)
    step_i = [0]
    unit_i = [0]
    stage_i = [0]
    def ga_load(g):
        s = g % 2
        P.dma("sp", lambda e, s=s, g=g: e.dma_start(out=Ksb[s], in_=qkv[(16 + g) * 128:(17 + g) * 128, :]),
              writes=[r_kvq[s]])
        P.dma("sp", lambda e, s=s, g=g: e.dma_start(
            out=Vsb[s], in_=C.vtok[:, g * 128:(g + 1) * 128].rearrange("(c p) d -> p c d", p=128)),
            writes=[r_kvq[s]])
        P.dma("sp", lambda e, s=s, g=g: e.dma_start(
            out=Qsb[s], in_=kview(qkv[g * 512:(g + 1) * 512, :])), writes=[r_kvq[s]])

    ga_load(0)
    for g in range(4):
        s = g % 2
        if g + 1 < 4:
            ga_load(g + 1)
        units = []
        for qc in range(2 if with_ctx else 0):
            units.append((qc, [(0, None), (1, None)]))
        for c in range(32):
            ks = [(0, None), (1, None)]
            if c > 0:
                ks.append((2 + c - 1, "prev"))
            ks.append((2 + c, None))
            if c < 31:
                ks.append((2 + c + 1, "next"))
            units.append((2 + c, ks))
        steps = []
        for ui, (qc, ks) in enumerate(units):
            for ki, (kc, mk) in enumerate(ks):
                steps.append((ui, qc, kc, mk, ki == 0, ki == len(ks) - 1))
        pend = None

        def s_mm(st, s=s):
            ui, qc, kc, mk, first, last = st
            sb = step_i[0] % 3
            pi = step_i[0] % NP
            step_i[0] += 1
            qm = Qsb[s][:, :, qc * 128:(qc + 1) * 128]
            outp = C.ps[:, sb, :].rearrange("p (h q) -> p h q", h=4)
            P.op("pe", lambda e: e.matmul(outp, Ksb[s][:, kc * 128:(kc + 1) * 128], qm, start=True, stop=True),
                 reads=[r_kvq[s]], writes=[C.rps[sb]])
            P.op("act", lambda e: e.activation(pt[pi], outp, AF.Exp), reads=[C.rps[sb]], writes=[r_pt[pi]])
            if mk is not None:
                m = mprev if mk == "prev" else mnext
                P.op("dve", lambda e: e.tensor_tensor(out=pt[pi], in0=pt[pi], in1=m, op=ALU.mult),
                     reads=[r_m], writes=[r_pt[pi]])
            return pi

        def pv(st, pi, s=s, g=g):
            ui, qc, kc, mk, first, last = st
            if first:
                unit_i[0] += 1
            u2 = unit_i[0] % 2
            bo, bd = 3 + u2, 5 + u2
            po = C.ps[:, bo, :].rearrange("p (h q) -> p h q", h=4)
            pd = C.ps[:, bd, :].rearrange("p (h q) -> p h q", h=4)

            def mm(e):
                e.matmul(po, Vsb[s][:, kc, :], pt[pi], start=first, stop=last)
                return e.matmul(pd, C.ones_b, pt[pi], start=first, stop=last)
            P.op("pe", mm, reads=[r_pt[pi], r_kvq[s], C.r_const], writes=[C.rps[bo], C.rps[bd]])
            if not last:
                return
            P.op("dve", lambda e: e.tensor_tensor(out=dn[u2], in0=pd, in1=esf[:, 4 * g:4 * g + 4, :], op=ALU.add),
                 reads=[C.rps[bd], r_m], writes=[r_dn[u2]])
            P.op("act", lambda e: e.activation(dn[u2], dn[u2], AF.Ln), reads=[r_dn[u2]], writes=[r_dn[u2]])
            P.op("act", lambda e: e.activation(rd[u2], dn[u2], AF.Exp, scale=-1.0), reads=[r_dn[u2]], writes=[r_rd[u2]])
            if qc < 2:
                so, flush, t0, tn = qc * 128, qc == 1, 0, 256
            else:
                c = qc - 2
                so, flush, t0, tn = (c % 4) * 128, c % 4 == 3, NCTX + (c // 4) * 512, 512
            si_ = stage_i[0] % 2
            P.op("dve", lambda e: e.tensor_tensor(out=ost[si_][:, :, so:so + 128], in0=po, in1=rd[u2], op=ALU.mult),
                 reads=[C.rps[bo], r_rd[u2]], writes=[r_ost[si_]])
            if flush:
                P.dma("sp", lambda e: e.dma_start(out=oTv[:, 4 * g:4 * g + 4, t0:t0 + tn], in_=ost[si_][:, :, 0:tn]),
                      reads=[r_ost[si_]])
                stage_i[0] += 1

        pq = []
        for st in steps:
            pi = s_mm(st)
            pq.append((st, pi))
            if len(pq) > 2:
                pv(*pq.pop(0))
        while pq:
            pv(*pq.pop(0))
    phase_begin(C)
    Wo = C.ga_wo[j]
    epi = make_resid_epi(C, lambda c, col: C.mod[:, 32 + c, col:col + 1], lambda si, ui: si * 4 + ui)
    supers = [([(Wo, sc * 512, 512)], [[0], [128], [256], [384]]) for sc in range(4)]
    wo_groups = HALF_GROUPS if with_ctx else [HALF_GROUPS[0][1:], HALF_GROUPS[1]]
    gemm(C, C.oT[0:D, :], KD, wo_groups, [dict(kind="B", supers=supers, epi=epi)], list(range(8)), 512)


def phase_df(C, l):
    P, A = C.P, C.A
    W = C.df_wqkv[0]
    lam_init = 0.8 - 0.6 * math.exp(-0.3 * l)
    phase_begin(C)
    cosT, sinT, r_tab = load_rope_tables(C, C.cos_ax, C.sin_ax)
    gv = A.alloc([2], F32)
    P.dma("sp", lambda e: e.dma_start(out=gv, in_=C.df_qkg), writes=[C.r_lay])
    P.op("dve", lambda e: e.tensor_scalar(out=gv[:, 0:1], in0=gv[:, 0:1], scalar1=128 ** -0.5, scalar2=None,
                                          op0=ALU.mult), reads=[C.r_lay], writes=[C.r_lay])
    qk_epi = make_qk_epi(C, lambda ch: gv[:, 0:1] if ch < 16 else gv[:, 1:2], C.qkT,
                         lambda si, ui: si * 4 + ui, cosT, sinT, r_tab)
    v_epi = make_v_epi(C, C.vtok, 2048)
    supers = [([(W, sc * 512, 512)], [[0], [128], [256], [384]]) for sc in range(8)]
    gemm(C, C.aT, KD, HALF_GROUPS,
         [dict(kind="B", supers=supers, epi=qk_epi),
          dict(kind="A", cols=[(W, 4096 + 512 * i) for i in range(4)], epi=v_epi)],
         [0, 1, 2, 3], 512)
    phase_begin(C)
    lam_t = A.alloc([4, 128], F32)
    pr = A.alloc([2, 128], F32)
    sm = A.alloc([2], F32)
    nlam = A.alloc([1], F32)
    r_l = Res()
    P.dma("sp", lambda e: e.dma_start(out=lam_t, in_=C.df_lam.rearrange("a d -> (a d)").partition_broadcast(128)),
          writes=[r_l])
    for i2 in range(2):
        P.op("dve", lambda e, i2=i2: e.tensor_tensor(out=pr[:, i2, :], in0=lam_t[:, 2 * i2, :],
                                                     in1=lam_t[:, 2 * i2 + 1, :], op=ALU.mult),
             reads=[r_l], writes=[r_l])
        P.op("dve", lambda e, i2=i2: e.tensor_reduce(out=sm[:, i2:i2 + 1], in_=pr[:, i2, :],
                                                     axis=mybir.AxisListType.X, op=ALU.add),
             reads=[r_l], writes=[r_l])
    P.op("act", lambda e: e.activation(sm, sm, AF.Exp), reads=[r_l], writes=[r_l])
    P.op("dve", lambda e: e.tensor_tensor(out=nlam, in0=sm[:, 1:2], in1=sm[:, 0:1], op=ALU.subtract),
         reads=[r_l], writes=[r_l])
    P.op("dve", lambda e: e.tensor_scalar(out=nlam, in0=nlam, scalar1=-lam_init, scalar2=None, op0=ALU.add),
         reads=[r_l], writes=[r_l])
    Ksb2 = [A.alloc([2, T], BF16) for _ in range(2)]
    Qsb2 = [A.alloc([2, T], BF16) for _ in range(2)]
    Vsb2 = [A.alloc([34, 256], BF16) for _ in range(2)]
    r_k2, r_q2, r_v2 = [Res(), Res()], [Res(), Res()], [Res(), Res()]
    pt0 = [A.alloc([512], BF16) for _ in range(2)]
    pt1 = [A.alloc([512], BF16) for _ in range(2)]
    r_pt = [Res(), Res()]
    r0t = A.alloc([512], F32)
    r1t = A.alloc([512], F32)
    t0t = A.alloc([512], F32)
    t1t = A.alloc([512], F32)
    r_r, r_t = Res(), Res()
    of = [A.alloc([2, 512], F32) for _ in range(2)]
    r_of = [Res(), Res()]
    oFv = kview(C.oF)
    step_i = [0]
    unit_n = [0]
    def df_load(h):
        b2 = h % 2
        P.dma("sp", lambda e: e.dma_start(out=Ksb2[b2], in_=kview(C.qkT[(16 + 2 * h) * 128:(18 + 2 * h) * 128, :])),
              writes=[r_k2[b2]])
        P.dma("sp", lambda e: e.dma_start(out=Qsb2[b2], in_=kview(C.qkT[(2 * h) * 128:(2 * h + 2) * 128, :])),
              writes=[r_q2[b2]])
        P.dma("sp", lambda e: e.dma_start(
            out=Vsb2[b2], in_=C.vtok[:, h * 256:(h + 1) * 256].rearrange("(c p) d -> p c d", p=128)),
            writes=[r_v2[b2]])

    df_load(0)
    for h in range(8):
        if h + 1 < 8:
            df_load(h + 1)
        Ksb, Qsb, Vsb = Ksb2[h % 2], Qsb2[h % 2], Vsb2[h % 2]
        r_k, r_q, r_v = r_k2[h % 2], r_q2[h % 2], r_v2[h % 2]
        units = [((0, 256), [0, 1])] + [(TILES[i], list(range(34))) for i in range(1, 9)]
        steps = []
        for (tile, ks) in units:
            for ki, kc in enumerate(ks):
                steps.append((tile, kc, ki == 0, ki == len(ks) - 1))

        def s_step(st, Ksb=Ksb, Qsb=Qsb, r_k=r_k, r_q=r_q):
            (off, n), kc, first, last = st
            sl = step_i[0] % 2
            step_i[0] += 1

            def mm(e):
                e.matmul(C.ps[:, 0, 0:n], Ksb[:, 0, kc * 128:(kc + 1) * 128], Qsb[:, 0, off:off + n],
                         start=True, stop=True)
                return e.matmul(C.ps[:, 1, 0:n], Ksb[:, 1, kc * 128:(kc + 1) * 128], Qsb[:, 1, off:off + n],
                                start=True, stop=True)
            P.op("pe", mm, reads=[r_k, r_q], writes=[C.rps[0], C.rps[1]])
            P.op("act", lambda e: e.activation(pt0[sl][:, 0:n], C.ps[:, 0, 0:n], AF.Exp),
                 reads=[C.rps[0]], writes=[r_pt[sl]])
            P.op("act", lambda e: e.activation(pt1[sl][:, 0:n], C.ps[:, 1, 0:n], AF.Exp),
                 reads=[C.rps[1]], writes=[r_pt[sl]])
            return sl

        def pv_step(st, sl, h=h, Vsb=Vsb, r_v=r_v):
            (off, n), kc, first, last = st

            def mm(e):
                for e2 in range(2):
                    e.matmul(C.ps[:, 2 + e2, 0:n], Vsb[:, kc, e2 * 128:(e2 + 1) * 128], pt0[sl][:, 0:n],
                             start=first, stop=last)
                e.matmul(C.ps[:, 6, 0:n], C.ones_b, pt0[sl][:, 0:n], start=first, stop=last)
                for e2 in range(2):
                    e.matmul(C.ps[:, 4 + e2, 0:n], Vsb[:, kc, e2 * 128:(e2 + 1) * 128], pt1[sl][:, 0:n],
                             start=first, stop=last)
                return e.matmul(C.ps[:, 7, 0:n], C.ones_b, pt1[sl][:, 0:n], start=first, stop=last)
            P.op("pe", mm, reads=[r_pt[sl], r_v, C.r_const], writes=[C.rps[b] for b in range(2, 8)])
            if not last:
                return
            ui = unit_n[0] % 2
            unit_n[0] += 1
            P.op("act", lambda e: e.activation(r0t[:, 0:n], C.ps[:, 6, 0:n], AF.Ln), reads=[C.rps[6]], writes=[r_r])
            P.op("act", lambda e: e.activation(r1t[:, 0:n], C.ps[:, 7, 0:n], AF.Ln), reads=[C.rps[7]], writes=[r_r])
            P.op("act", lambda e: e.activation(r0t[:, 0:n], r0t[:, 0:n], AF.Exp, scale=-1.0), reads=[r_r], writes=[r_r])
            P.op("act", lambda e: e.activation(r1t[:, 0:n], r1t[:, 0:n], AF.Exp, scale=-1.0), reads=[r_r], writes=[r_r])
            for e2 in range(2):
                P.op("dve", lambda e, e2=e2: e.tensor_tensor(out=t0t[:, 0:n], in0=C.ps[:, 2 + e2, 0:n],
                                                             in1=r0t[:, 0:n], op=ALU.mult),
                     reads=[C.rps[2 + e2], r_r], writes=[r_t])
                P.op("dve", lambda e, e2=e2: e.tensor_tensor(out=t1t[:, 0:n], in0=C.ps[:, 4 + e2, 0:n],
                                                             in1=r1t[:, 0:n], op=ALU.mult),
                     reads=[C.rps[4 + e2], r_r], writes=[r_t])
                P.op("dve", lambda e, e2=e2: e.scalar_tensor_tensor(out=of[ui][:, e2, 0:n], in0=t1t[:, 0:n],
                                                                    scalar=nlam[:, 0:1], in1=t0t[:, 0:n],
                                                                    op0=ALU.mult, op1=ALU.add),
                     reads=[r_t, r_l], writes=[r_of[ui]])
            P.dma("sp", lambda e: e.dma_start(out=oFv[:, 2 * h:2 * h + 2, off:off + n], in_=of[ui][:, :, 0:n]),
                  reads=[r_of[ui]])

        pend = None
        for st in steps:
            sl = s_step(st)
            if pend is not None:
                pv_step(*pend)
            pend = (st, sl)
        pv_step(*pend)
    phase_begin(C)
    sg = A.alloc([2], F32)
    r_sg = Res()
    P.dma("sp", lambda e: e.dma_start(out=sg, in_=C.df_subln), writes=[r_sg])
    P.op("dve", lambda e: e.tensor_scalar(out=sg, in0=sg, scalar1=1.0 - lam_init, scalar2=None, op0=ALU.mult),
         reads=[r_sg], writes=[r_sg])
    ot = [A.alloc([KD, 512], F32) for _ in range(2)]
    sq = [A.alloc([KD, 512], BF16) for _ in range(2)]
    at = [A.alloc([KD, 512], BF16) for _ in range(2)]
    r_ot, r_sq, r_at = [Res(), Res()], [Res(), Res()], [Res(), Res()]
    rs = [A.alloc([512], F32) for _ in range(2)]
    rstd = [A.alloc([512], F32) for _ in range(2)]
    xn = [A.alloc([512], F32) for _ in range(4)]
    r_rs, r_rstd = [Res(), Res()], [Res(), Res()]
    r_xn = [Res() for _ in range(4)]
    o2v = kview(C.oT[0:D, :])
    xi = 0
    hi = 0
    for ti, (off, n) in enumerate(TILES):
        s = ti % 2
        P.dma("sp", lambda e, s=s, off=off, n=n: e.dma_start(out=ot[s][:, :, 0:n], in_=oFv[:, :, off:off + n]),
              writes=[r_ot[s]])
        P.op("act", lambda e, s=s, n=n: e.activation(sq[s][:, :, 0:n], ot[s][:, :, 0:n], AF.Square),
             reads=[r_ot[s]], writes=[r_sq[s]])
        for h in range(8):
            def mm(e, s=s, n=n, h=h):
                e.matmul(C.ps[:, h, 0:n], C.ones_b, sq[s][:, 2 * h, 0:n], start=True, stop=False)
                return e.matmul(C.ps[:, h, 0:n], C.ones_b, sq[s][:, 2 * h + 1, 0:n], start=False, stop=True)
            P.op("pe", mm, reads=[r_sq[s], C.r_const], writes=[C.rps[h]])
        for h in range(8):
            h2 = hi % 2
            hi += 1
            P.op("act", lambda e, h=h, h2=h2, n=n: e.activation(rs[h2][:, 0:n], C.ps[:, h, 0:n], AF.Sqrt,
                                                                scale=1.0 / 256, bias=C.eps_t[:, 0:1]),
                 reads=[C.rps[h], C.r_const], writes=[r_rs[h2]])
            P.op("dve", lambda e, h2=h2, n=n: e.reciprocal(rstd[h2][:, 0:n], rs[h2][:, 0:n]),
                 reads=[r_rs[h2]], writes=[r_rstd[h2]])
            for e2 in range(2):
                x4 = xi % 4
                xi += 1
                k = 2 * h + e2
                P.op("dve", lambda e, s=s, k=k, n=n, x4=x4, h2=h2: e.tensor_tensor(
                    out=xn[x4][:, 0:n], in0=ot[s][:, k, 0:n], in1=rstd[h2][:, 0:n], op=ALU.mult),
                    reads=[r_ot[s], r_rstd[h2]], writes=[r_xn[x4]])
                P.op("act", lambda e, s=s, k=k, n=n, x4=x4, e2=e2: e.activation(
                    at[s][:, k, 0:n], xn[x4][:, 0:n], AF.Identity, scale=sg[:, e2:e2 + 1]),
                    reads=[r_xn[x4], r_sg], writes=[r_at[s]])
        P.dma("sp", lambda e, s=s, off=off, n=n: e.dma_start(out=o2v[:, :, off:off + n], in_=at[s][:, :, 0:n]),
              reads=[r_at[s]])
    phase_begin(C)
    Wo = C.df_wo[0]
    epi = make_resid_epi(C, lambda c, col: C.mod[:, 32 + c, col:col + 1], lambda si, ui: si * 4 + ui)
    supers = [([(Wo, sc * 512, 512)], [[0], [128], [256], [384]]) for sc in range(4)]
    gemm(C, C.oT[0:D, :], KD, HALF_GROUPS, [dict(kind="B", supers=supers, epi=epi)], list(range(8)), 512)


def phase_rt(C, l):
    P, A = C.P, C.A
    W = C.rt_w_in[0]
    phase_begin(C)
    cosT, sinT, r_tab = load_rope_tables(C, C.rt_cos, C.rt_sin)
    NB = 2
    ta = [A.alloc([512], F32) for _ in range(NB)]
    tb = [A.alloc([512], F32) for _ in range(NB)]
    tc_ = [A.alloc([512], F32) for _ in range(NB)]
    td = [A.alloc([512], F32) for _ in range(NB)]
    ob = [A.alloc([2, 512], BF16) for _ in range(NB)]
    RR = lambda: [Res() for _ in range(NB)]
    r_ta, r_tb, r_tc, r_td, r_ob = RR(), RR(), RR(), RR(), RR()
    cnt = [0]

    def qk_epi(si, ui, tile, banks, ti, nt):
        off, n = tile
        hh = si * 2 + ui
        scl = 256 ** -0.5 if hh < 8 else 1.0
        b0, b1 = banks
        i = cnt[0] % NB
        cnt[0] += 1
        if off >= NCTX:
            lo = off - NCTX
            P.op("dve", lambda e: e.tensor_tensor(out=ta[i][:, 0:n], in0=C.ps[:, b0, 0:n], in1=cosT[:, lo:lo + n],
                                                  op=ALU.mult), reads=[C.rps[b0], r_tab], writes=[r_ta[i]])
            P.op("dve", lambda e: e.tensor_tensor(out=tb[i][:, 0:n], in0=C.ps[:, b1, 0:n], in1=sinT[:, lo:lo + n],
                                                  op=ALU.mult), reads=[C.rps[b1], r_tab], writes=[r_tb[i]])
            P.op("pool", lambda e: e.tensor_tensor(out=ta[i][:, 0:n], in0=ta[i][:, 0:n], in1=tb[i][:, 0:n],
                                                   op=ALU.subtract), reads=[r_tb[i]], writes=[r_ta[i]])
            P.op("act", lambda e: e.activation(ob[i][:, 0, 0:n], ta[i][:, 0:n], AF.Identity, scale=scl),
                 reads=[r_ta[i]], writes=[r_ob[i]])
            P.op("dve", lambda e: e.tensor_tensor(out=tc_[i][:, 0:n], in0=C.ps[:, b0, 0:n], in1=sinT[:, lo:lo + n],
                                                  op=ALU.mult), reads=[C.rps[b0], r_tab], writes=[r_tc[i]])
            P.op("dve", lambda e: e.tensor_tensor(out=td[i][:, 0:n], in0=C.ps[:, b1, 0:n], in1=cosT[:, lo:lo + n],
                                                  op=ALU.mult), reads=[C.rps[b1], r_tab], writes=[r_td[i]])
            P.op("pool", lambda e: e.tensor_tensor(out=tc_[i][:, 0:n], in0=tc_[i][:, 0:n], in1=td[i][:, 0:n],
                                                   op=ALU.add), reads=[r_td[i]], writes=[r_tc[i]])
            P.op("act", lambda e: e.activation(ob[i][:, 1, 0:n], tc_[i][:, 0:n], AF.Identity, scale=scl),
                 reads=[r_tc[i]], writes=[r_ob[i]])
        else:
            P.op("act", lambda e: e.activation(ob[i][:, 0, 0:n], C.ps[:, b0, 0:n], AF.Identity, scale=scl),
                 reads=[C.rps[b0]], writes=[r_ob[i]])
            P.op("act", lambda e: e.activation(ob[i][:, 1, 0:n], C.ps[:, b1, 0:n], AF.Identity, scale=scl),
                 reads=[C.rps[b1]], writes=[r_ob[i]])
        P.dma("sp", lambda e: e.dma_start(out=kview(C.qkT[hh * 256:(hh + 1) * 256, :])[:, :, off:off + n],
                                          in_=ob[i][:, :, 0:n]), reads=[r_ob[i]])
        return None

    v_epi = make_v_epi(C, C.vtok, 4096)
    supers = [([(W, sc * 512, 512)], [[0, 128], [256, 384]]) for sc in range(8)]
    gemm(C, C.aT, KD, HALF_GROUPS,
         [dict(kind="B", supers=supers, epi=qk_epi),
          dict(kind="A", cols=[(W, 4096 + 512 * i) for i in range(8)], epi=v_epi)],
         list(range(8)), 512)

    phase_begin(C)
    dl = A.alloc([16], F32)
    lg = A.alloc([16], F32)
    relm = A.alloc([2, 128], F32)
    m01 = A.alloc([2, 128], F32)
    qexp = A.alloc([2, 128], F32)
    kexp = A.alloc([2], F32)
    gn = A.alloc([2, 32], F32)
    ident_b = A.alloc([128], BF16)
    kdec = A.alloc([16], F32)
    cdec = A.alloc([16], F32)
    r_p = Res()
    P.dma("sp", lambda e: e.dma_start(out=dl, in_=C.rt_decay.partition_broadcast(128)), writes=[r_p])
    P.dma("sp", lambda e: e.dma_start(out=relm, in_=C.rt_rel), writes=[r_p])
    P.dma("sp", lambda e: e.dma_start(out=m01, in_=C.rt_m01), writes=[r_p])
    P.dma("sp", lambda e: e.dma_start(out=qexp, in_=C.rt_qexp), writes=[r_p])
    P.dma("sp", lambda e: e.dma_start(out=kexp, in_=C.rt_kexp), writes=[r_p])
    P.dma("sp", lambda e: e.dma_start(out=gn, in_=C.rt_gn), writes=[r_p])
    P.dma("pool", lambda e: e.dma_start(out=ident_b, in_=C.ident_d), writes=[r_p])
    P.op("act", lambda e: e.activation(dl, dl, AF.Exp, scale=-1.0), reads=[r_p], writes=[r_p])
    P.op("act", lambda e: e.activation(lg, dl, AF.Ln, bias=C.one_t[:, 0:1]), reads=[r_p, C.r_const], writes=[r_p])
    P.op("dve", lambda e: e.tensor_scalar(out=lg, in0=lg, scalar1=-1.0, scalar2=None, op0=ALU.mult),
         reads=[r_p], writes=[r_p])
    for dh in range(16):
        d_ = dh // 8
        P.op("act", lambda e, dh=dh, d_=d_: e.activation(kdec[:, dh:dh + 1], kexp[:, d_:d_ + 1], AF.Exp,
                                                         scale=lg[:, dh:dh + 1]), reads=[r_p], writes=[r_p])
    P.op("act", lambda e: e.activation(cdec, lg, AF.Exp, scale=128.0), reads=[r_p], writes=[r_p])

    qT = A.alloc([2, T], BF16)
    kT = A.alloc([2, T], BF16)
    Vsb = A.alloc([34, 512], BF16)
    kd = [A.alloc([34, 256], BF16) for _ in range(2)]
    r_q, r_k, r_v, r_kd = Res(), Res(), Res(), Res()
    maskt = A.alloc([2, 128], F32)
    qdect = A.alloc([2, 128], F32)
    r_hd = Res()
    S = [A.alloc([2, 512], F32) for _ in range(2)]
    Sbf = [[A.alloc([2, 512], BF16) for _ in range(2)] for _ in range(2)]
    r_S = [Res(), Res()]
    r_Sbf = [[Res(), Res()], [Res(), Res()]]
    oacc = [[A.alloc([4, 512], F32) for _ in range(2)] for _ in range(2)]
    r_oacc = [[Res(), Res()], [Res(), Res()]]
    obf2 = [A.alloc([4, 256], BF16) for _ in range(2)]
    osq2 = [A.alloc([4, 256], BF16) for _ in range(2)]
    r_obf2 = [Res(), Res()]
    mean = A.alloc([256], F32)
    msq = A.alloc([256], F32)
    var = A.alloc([256], F32)
    rs = A.alloc([256], F32)
    rstd = A.alloc([256], F32)
    r_st = Res()
    tt = [A.alloc([256], F32) for _ in range(2)]
    r_tt = [Res(), Res()]
    yb = [A.alloc([4, 256], BF16) for _ in range(2)]
    r_yb = [Res(), Res()]
    pmb = [A.alloc([128], BF16) for _ in range(2)]
    r_pm = [Res(), Res()]
    qd = [A.alloc([2, 128], BF16) for _ in range(2)]
    r_qd = [Res(), Res()]
    r_in = [Res() for _ in range(4)]
    psb = C.ps.rearrange("p b f -> p (b f)").bitcast(BF16).rearrange("p (b f) -> p b f", b=8)
    orders = [list(range(34)), [1, 0] + list(range(33, 1, -1))]
    slot_i = [0]
    yb_i = [0]
    tt_i = [0]

    for h in range(8):
        P.dma("sp", lambda e, h=h: e.dma_start(out=qT, in_=kview(C.qkT[h * 256:(h + 1) * 256, :])), writes=[r_q])
        P.dma("sp", lambda e, h=h: e.dma_start(out=kT, in_=kview(C.qkT[(8 + h) * 256:(9 + h) * 256, :])),
              writes=[r_k])
        P.dma("sp", lambda e, h=h: e.dma_start(
            out=Vsb, in_=C.vtok[:, h * 512:(h + 1) * 512].rearrange("(c p) d -> p c d", p=128)), writes=[r_v])
        for d_ in range(2):
            dh = d_ * 8 + h
            P.op("act", lambda e, d_=d_, dh=dh: e.activation(maskt[:, d_, :], relm[:, d_, :], AF.Exp,
                                                             scale=lg[:, dh:dh + 1]), reads=[r_p], writes=[r_hd])
            P.op("dve", lambda e, d_=d_: e.tensor_tensor(out=maskt[:, d_, :], in0=maskt[:, d_, :], in1=m01[:, d_, :],
                                                         op=ALU.mult), reads=[r_p], writes=[r_hd])
            P.op("act", lambda e, d_=d_, dh=dh: e.activation(qdect[:, d_, :], qexp[:, d_, :], AF.Exp,
                                                             scale=lg[:, dh:dh + 1]), reads=[r_p], writes=[r_hd])
            P.op("pool", lambda e, d_=d_: e.memset(S[d_], 0.0), writes=[r_S[d_]])
            P.op("pool", lambda e, d_=d_: e.memset(Sbf[d_][0], 0.0), writes=[r_Sbf[d_][0]])
        for cg in range(17):
            def tr(e, cg=cg):
                for cc in range(2):
                    for dc in range(2):
                        c = 2 * cg + cc
                        i = e.transpose(psb[:, 7, (cc * 2 + dc) * 128:(cc * 2 + dc + 1) * 128],
                                        kT[:, dc, c * 128:(c + 1) * 128], ident_b)
                return i
            P.op("pe", tr, reads=[r_k, r_p], writes=[C.rps[7]])
            src = psb[:, 7, 0:512].rearrange("p (c f) -> p c f", c=2)
            P.op("act", lambda e, cg=cg, src=src, h=h: e.activation(kd[0][:, 2 * cg:2 * cg + 2, :], src, AF.Identity,
                                                                    scale=kdec[:, h:h + 1]),
                 reads=[C.rps[7], r_p], writes=[r_kd])
            P.op("dve", lambda e, cg=cg, src=src, h=h: e.tensor_scalar(out=kd[1][:, 2 * cg:2 * cg + 2, :], in0=src,
                                                                       scalar1=kdec[:, 8 + h:9 + h], scalar2=None,
                                                                       op0=ALU.mult),
                 reads=[C.rps[7], r_p], writes=[r_kd])

        pending = []
        tile_cnt = {}
        obuf = [0, 0]

        def group_norm(d_, tile_i, buf, h=h):
            off, n = TILES[tile_i]
            for hs in range(0, n, 256):
                oa = oacc[d_][buf]
                obf, osq, r_obf = obf2[hs // 256], osq2[hs // 256], r_obf2[hs // 256]
                P.op("act", lambda e, oa=oa, hs=hs, obf=obf: e.activation(obf, oa[:, :, hs:hs + 256], AF.Identity),
                     reads=[r_oacc[d_][buf]], writes=[r_obf])
                P.op("act", lambda e, oa=oa, hs=hs, osq=osq: e.activation(osq, oa[:, :, hs:hs + 256], AF.Square),
                     reads=[r_oacc[d_][buf]], writes=[r_obf])

                def later(oa=oa, hs=hs, off=off, obf=obf, osq=osq, r_obf=r_obf):
                    def mm(e):
                        for ec in range(4):
                            e.matmul(C.ps[:, 7, 0:256], C.ones_b, obf[:, ec, :], start=(ec == 0), stop=(ec == 3))
                        for ec in range(4):
                            i = e.matmul(C.ps[:, 7, 256:512], C.ones_b, osq[:, ec, :], start=(ec == 0), stop=(ec == 3))
                        return i
                    P.op("pe", mm, reads=[r_obf, C.r_const], writes=[C.rps[7]])
                    P.op("dve", lambda e: e.tensor_scalar(out=mean, in0=C.ps[:, 7, 0:256], scalar1=1.0 / 512,
                                                          scalar2=None, op0=ALU.mult),
                         reads=[C.rps[7]], writes=[r_st])
                    P.op("dve", lambda e: e.tensor_tensor(out=msq, in0=mean, in1=mean, op=ALU.mult),
                         reads=[r_st], writes=[r_st])
                    P.op("dve", lambda e: e.scalar_tensor_tensor(out=var, in0=C.ps[:, 7, 256:512], scalar=1.0 / 512,
                                                                 in1=msq, op0=ALU.mult, op1=ALU.subtract),
                         reads=[C.rps[7], r_st], writes=[r_st])
                    P.op("dve", lambda e: e.tensor_scalar(out=var, in0=var, scalar1=0.0, scalar2=None, op0=ALU.max),
                         reads=[r_st], writes=[r_st])
                    P.op("act", lambda e: e.activation(rs, var, AF.Ln, bias=C.eps_t[:, 0:1]),
                         reads=[r_st, C.r_const], writes=[r_st])
                    P.op("act", lambda e: e.activation(rstd, rs, AF.Exp, scale=-0.5), reads=[r_st], writes=[r_st])
                    yi = yb_i[0] % 2
                    yb_i[0] += 1
                    for ec in range(4):
                        ti_ = tt_i[0] % 2
                        tt_i[0] += 1
                        P.op("dve", lambda e, ec=ec, ti_=ti_: e.tensor_tensor(out=tt[ti_], in0=oa[:, ec, hs:hs + 256],
                                                                              in1=mean, op=ALU.subtract),
                             reads=[r_oacc[d_][buf], r_st], writes=[r_tt[ti_]])
                        P.op("dve", lambda e, ti_=ti_: e.tensor_tensor(out=tt[ti_], in0=tt[ti_], in1=rstd, op=ALU.mult),
                             reads=[r_st], writes=[r_tt[ti_]])
                        P.op("act", lambda e, ec=ec, ti_=ti_, yi=yi: e.activation(
                            yb[yi][:, ec, :], tt[ti_], AF.Identity, scale=gn[:, d_, h * 4 + ec:h * 4 + ec + 1]),
                            reads=[r_tt[ti_], r_p], writes=[r_yb[yi]])
                    P.dma("sp", lambda e, yi=yi: e.dma_start(
                        out=kview(C.yT[d_][h * 512:(h + 1) * 512, :])[:, :, off + hs:off + hs + 256], in_=yb[yi]),
                        reads=[r_yb[yi]])
                pending.append(later)

        for s_ in range(34):
            for d_ in range(2):
                dh = d_ * 8 + h
                c = orders[d_][s_]
                r = (2 * s_ + d_) % 4
                sl = slot_i[0] % 2
                slot_i[0] += 1
                cur, nxt = s_ % 2, (s_ + 1) % 2
                ps_in = C.ps[:, 0, r * 128:(r + 1) * 128]

                def mm_in(e, c=c, ps_in=ps_in):
                    e.matmul(ps_in, kT[:, 0, c * 128:(c + 1) * 128], qT[:, 0, c * 128:(c + 1) * 128],
                             start=True, stop=False)
                    return e.matmul(ps_in, kT[:, 1, c * 128:(c + 1) * 128], qT[:, 1, c * 128:(c + 1) * 128],
                                    start=False, stop=True)
                P.op("pe", mm_in, reads=[r_k, r_q], writes=[r_in[r]])
                P.op("dve", lambda e, sl=sl, ps_in=ps_in, d_=d_: e.tensor_tensor(out=pmb[sl], in0=ps_in,
                                                                                 in1=maskt[:, d_, :], op=ALU.mult),
                     reads=[r_in[r], r_hd], writes=[r_pm[sl]])
                for dc in range(2):
                    P.op("pool", lambda e, sl=sl, dc=dc, c=c, d_=d_: e.tensor_tensor(
                        out=qd[sl][:, dc, :], in0=qT[:, dc, c * 128:(c + 1) * 128], in1=qdect[:, d_, :], op=ALU.mult),
                        reads=[r_q, r_hd], writes=[r_qd[sl]])
                bo = 1 + d_
                bu = 3 + 2 * d_

                def mm_u(e, c=c, d_=d_, bu=bu):
                    e.matmul(C.ps[:, bu, :], kd[d_][:, c, 0:128], Vsb[:, c, :], start=True, stop=True)
                    return e.matmul(C.ps[:, bu + 1, :], kd[d_][:, c, 128:256], Vsb[:, c, :], start=True, stop=True)
                P.op("pe", mm_u, reads=[r_kd, r_v], writes=[C.rps[bu], C.rps[bu + 1]])

                def mm_o(e, c=c, d_=d_, bo=bo, sl=sl, cur=cur):
                    for ec in range(4):
                        o_ = C.ps[:, bo, ec * 128:(ec + 1) * 128]
                        e.matmul(o_, Sbf[d_][cur][:, 0, ec * 128:(ec + 1) * 128], qd[sl][:, 0, :], start=True, stop=False)
                        e.matmul(o_, Sbf[d_][cur][:, 1, ec * 128:(ec + 1) * 128], qd[sl][:, 1, :], start=False, stop=False)
                        i = e.matmul(o_, Vsb[:, c, ec * 128:(ec + 1) * 128], pmb[sl], start=False, stop=True)
                    return i
                P.op("pe", mm_o, reads=[r_Sbf[d_][cur], r_qd[sl], r_pm[sl], r_v], writes=[C.rps[bo]])
                todo, pending[:] = list(pending), []
                for fn in todo:
                    fn()
                P.op("dve", lambda e, d_=d_, dh=dh, bu=bu: e.scalar_tensor_tensor(
                    out=S[d_], in0=S[d_], scalar=cdec[:, dh:dh + 1], in1=C.ps[:, bu:bu + 2, :], op0=ALU.mult,
                    op1=ALU.add), reads=[C.rps[bu], C.rps[bu + 1], r_p], writes=[r_S[d_]])
                P.op("act", lambda e, d_=d_, nxt=nxt: e.activation(Sbf[d_][nxt], S[d_], AF.Identity),
                     reads=[r_S[d_]], writes=[r_Sbf[d_][nxt]])
                if c < 2:
                    tile_i, pos, need = 0, c, 2
                else:
                    tile_i, pos, need = 1 + (c - 2) // 4, (c - 2) % 4, 4
                buf = obuf[d_]
                src_o = C.ps[:, bo, :].rearrange("p (a q) -> p a q", a=4)
                P.op("pool" if False else "act", lambda e, d_=d_, buf=buf, pos=pos, src_o=src_o: e.activation(
                    oacc[d_][buf][:, :, pos * 128:(pos + 1) * 128], src_o, AF.Identity),
                    reads=[C.rps[bo]], writes=[r_oacc[d_][buf]])
                key = (d_, tile_i)
                tile_cnt[key] = tile_cnt.get(key, 0) + 1
                if tile_cnt[key] == need:
                    group_norm(d_, tile_i, buf)
                    obuf[d_] = 1 - buf
        for fn in pending:
            fn()
        pending[:] = []

    phase_begin(C)
    NB3 = 3
    yf_t = [A.alloc([512], BF16) for _ in range(NB3)]
    yb_t = [A.alloc([512], BF16) for _ in range(NB3)]
    sf = [A.alloc([512], F32) for _ in range(2)]
    sb_ = [A.alloc([512], F32) for _ in range(2)]
    yc = [A.alloc([512], BF16) for _ in range(2)]
    r_y3 = [Res() for _ in range(NB3)]
    r_sf, r_sb, r_yc = [Res(), Res()], [Res(), Res()], [Res(), Res()]
    gcnt = [0]

    def g_epi(si, ui, tile, banks, ti, nt):
        off, n = tile
        j = si * 2 + ui
        bf_, bb_ = banks
        i3 = gcnt[0] % NB3
        i2 = gcnt[0] % 2
        gcnt[0] += 1
        P.dma("sp", lambda e: e.dma_start(out=yf_t[i3][:, 0:n], in_=C.yT[0][j * 128:(j + 1) * 128, off:off + n]),
              writes=[r_y3[i3]])
        P.dma("sp", lambda e: e.dma_start(out=yb_t[i3][:, 0:n], in_=C.yT[1][j * 128:(j + 1) * 128, off:off + n]),
              writes=[r_y3[i3]])
        P.op("act", lambda e: e.activation(sf[i2][:, 0:n], C.ps[:, bf_, 0:n], AF.Silu),
             reads=[C.rps[bf_]], writes=[r_sf[i2]])
        P.op("act", lambda e: e.activation(sb_[i2][:, 0:n], C.ps[:, bb_, 0:n], AF.Silu),
             reads=[C.rps[bb_]], writes=[r_sb[i2]])
        P.op("dve", lambda e: e.tensor_tensor(out=sf[i2][:, 0:n], in0=sf[i2][:, 0:n], in1=yf_t[i3][:, 0:n],
                                              op=ALU.mult), reads=[r_y3[i3]], writes=[r_sf[i2]])
        P.op("dve", lambda e: e.tensor_tensor(out=sb_[i2][:, 0:n], in0=sb_[i2][:, 0:n], in1=yb_t[i3][:, 0:n],
                                              op=ALU.mult), reads=[r_y3[i3]], writes=[r_sb[i2]])
        P.op("pool", lambda e: e.tensor_tensor(out=yc[i2][:, 0:n], in0=sf[i2][:, 0:n], in1=sb_[i2][:, 0:n],
                                               op=ALU.add), reads=[r_sf[i2], r_sb[i2]], writes=[r_yc[i2]])
        P.dma("sp", lambda e: e.dma_start(out=C.oT[j * 128:(j + 1) * 128, off:off + n], in_=yc[i2][:, 0:n]),
              reads=[r_yc[i2]])
        return None

    supers = [([(W, 8192 + sc * 256, 256), (W, 12288 + sc * 256, 256)], [[0, 256], [128, 384]]) for sc in range(16)]
    gemm(C, C.aT, KD, HALF_GROUPS, [dict(kind="B", supers=supers, epi=g_epi)], list(range(8)), 512)

    phase_begin(C)
    Wo = C.rt_wo[0]
    epi = make_resid_epi(C, lambda c, col: C.mod[:, 32 + c, col:col + 1], lambda si, ui: si * 2 + ui)
    supers = [([(Wo, sc * 256, 256)], [[0], [128]]) for sc in range(8)]
    gemm(C, C.oT, 32, HALF_GROUPS, [dict(kind="B", supers=supers, epi=epi)], list(range(8)), 256)


def build_program(nl=NL):
    nc = bass.Bass("TRN2", target_bir_lowering=False)
    C = Ctx()
    C.nc = nc
    C.P = Prog(nc)

    def din(name, shape, dt=F32):
        return nc.dram_tensor(name, list(shape), dt, kind="ExternalInput").ap()

    C.x_in = din("x", [NLAT, D])
    C.ctx_in = din("ctx", [NCTX, D])
    C.cvec = din("cvec", [128, KD, 2])
    C.modb_d = din("modb", [128, 4, 96])
    C.normg_d = din("normg", [128, 4, 2, 16])
    C.convw = din("convw", [128, 4, 88, 3])
    C.convb = din("convb", [128, 4, 88])
    C.mod_w = din("mod_w", [4, D, 6 * D])
    C.ffn_w_in = din("ffn_w_in", [4, D, 2 * DFF])
    C.ffn_w_out = din("ffn_w_out", [4, DFF, D])
    C.ga_wqkv = din("ga_wqkv", [2, D, 3072])
    C.ga_wo = din("ga_wo", [2, D, D])
    C.ga_qkg = din("ga_qkg", [128, 2, 2])
    C.ga_sink = din("ga_sink", [2, 16])
    C.df_wqkv = din("df_wqkv", [1, D, 6144])
    C.df_wo = din("df_wo", [1, D, D])
    C.df_qkg = din("df_qkg", [128, 2])
    C.df_subln = din("df_subln", [128, 2])
    C.df_lam = din("df_lam", [4, 128])
    C.rt_w_in = din("rt_w_in", [1, D, 16384])
    C.rt_wo = din("rt_wo", [1, 4096, D])
    C.rt_decay = din("rt_decay", [16])
    C.rt_gn = din("rt_gn", [128, 2, 32])
    C.rt_cos = din("rt_cos", [128, NLAT])
    C.rt_sin = din("rt_sin", [128, NLAT])
    C.rt_rel = din("rt_rel", [128, 2, 128])
    C.rt_m01 = din("rt_m01", [128, 2, 128])
    C.rt_qexp = din("rt_qexp", [128, 2, 128])
    C.rt_kexp = din("rt_kexp", [128, 2])
    C.ident_d = din("ident", [128, 128])
    C.rotm_d = din("rotm", [128, 128])
    C.cos_ax = din("cos_ax", [128, NLAT])
    C.sin_ax = din("sin_ax", [128, NLAT])
    C.mask_prev = din("mask_prev", [128, 4, 128])
    C.mask_next = din("mask_next", [128, 4, 128])
    C.out = nc.dram_tensor("out", [NLAT, D], F32, kind="ExternalOutput").ap()
    def scratch(name, shape, dt):
        if name in DEBUG_OUT:
            return nc.dram_tensor(name, shape, dt, kind="ExternalOutput").ap()
        return nc.dram_tensor(name, shape, dt).ap()
    C.hT = scratch("hT", [D, T], F32)
    C.aT = scratch("aT", [D, T], BF16)
    C.gT = scratch("gT", [DFF, T], BF16)
    C.qkT = scratch("qkT", [32 * 128, T], BF16)
    C.vtok = scratch("vtok", [T, 4096], BF16)
    C.oT = scratch("oT", [4096, T], BF16)
    C.oF = scratch("oF", [D, T], F32)
    C.yT = scratch("yT", [2, 4096, T], BF16)

    C.rt_as_df = RT_AS_DF
    C.A = Arena(nc, 206 * 1024)
    A, P = C.A, C.P
    C.ps = nc.alloc_psum_tensor("ps", [128, 8, 512], F32).ap()
    C.rps = [Res() for _ in range(8)]
    C.r_const = Res()
    C.r_mod = Res()
    C.r_lay = Res()
    C.ident_f = A.alloc([128], F32)
    C.ones_b = A.alloc([128], BF16)
    C.rotm_b = A.alloc([128], BF16)
    C.eps_t = A.alloc([1], F32)
    C.one_t = A.alloc([1], F32)
    C.mod = A.alloc([96, 2], F32)
    C.gs = A.alloc([2, 16, 2], F32)
    C.sc_b = A.alloc([KD, 2], BF16)
    C.modb = A.alloc([4, 96], F32)
    C.normg = A.alloc([4, 2, 16], F32)
    cv = A.alloc([KD, 2], F32)
    C.persist_off = A.off
    P.dma("sp", lambda e: e.dma_start(out=C.ident_f, in_=C.ident_d), writes=[C.r_const])
    P.dma("pool", lambda e: e.dma_start(out=C.rotm_b, in_=C.rotm_d), writes=[C.r_const])
    P.dma("sp", lambda e: e.dma_start(out=C.modb, in_=C.modb_d), writes=[C.r_const])
    P.dma("sp", lambda e: e.dma_start(out=C.normg, in_=C.normg_d), writes=[C.r_const])
    P.dma("sp", lambda e: e.dma_start(out=cv, in_=C.cvec), writes=[C.r_const])
    P.op("dve", lambda e: e.memset(C.ones_b, 1.0), writes=[C.r_const])
    P.op("dve", lambda e: e.memset(C.eps_t, EPS), writes=[C.r_const])
    P.op("dve", lambda e: e.memset(C.one_t, 1.0), writes=[C.r_const])
    P.op("act", lambda e: e.activation(C.sc_b, cv, AF.Silu), reads=[C.r_const], writes=[C.r_const])

    phase_input(C)
    for l in range(nl):
        kind, j = l % 3, l // 3
        phase_adaln(C, l)
        phase_norm(C, l, 0)
        need_ctx = l < 3
        if kind == 0:
            phase_ga(C, l, j, with_ctx=need_ctx)
        elif kind == 2:
            phase_df(C, l)
        elif C.rt_as_df:
            phase_df(C, l)
        else:
            phase_rt(C, l)
        if STOP_AFTER_MIXER == l:
            break
        phase_norm(C, l, 1, with_ctx=need_ctx)
        phase_ffn(C, l, with_ctx=need_ctx)
    phase_output(C)
    P.emit()
    return nc


def host_consts():
    ident = np.eye(128, dtype=np.float32)
    rotm = np.zeros((128, 128), np.float32)
    for m in range(64):
        rotm[m + 64, m] = -1.0
        rotm[m, m + 64] = 1.0
    rows = np.repeat(np.arange(64), 64).astype(np.float32)
    cols = np.tile(np.arange(64), 64).astype(np.float32)
    nf = 32
    inv = (np.float32(10000.0) ** (-np.arange(nf, dtype=np.float32) / nf)).astype(np.float32)
    ang = np.concatenate([rows[:, None] * inv, cols[:, None] * inv], -1).astype(np.float32)
    cos, sin = np.cos(ang).astype(np.float32), np.sin(ang).astype(np.float32)
    cos_ax = np.ascontiguousarray(np.concatenate([cos, cos], -1).T)
    sin_ax = np.ascontiguousarray(np.concatenate([sin, sin], -1).T)
    jj = np.arange(128)[:, None]
    ii = np.arange(128)[None, :]
    mp = (jj >= ii).astype(np.float32)
    mn = (jj <= ii).astype(np.float32)
    mask_prev = np.ascontiguousarray(np.repeat(mp[:, None, :], 4, 1))
    mask_next = np.ascontiguousarray(np.repeat(mn[:, None, :], 4, 1))
    invr = (np.float32(10000.0) ** (-np.arange(128, dtype=np.float32) / 128)).astype(np.float32)
    angr = (np.arange(NLAT, dtype=np.float32)[:, None] * invr).astype(np.float32)
    rt_cos = np.ascontiguousarray(np.cos(angr).astype(np.float32).T)
    rt_sin = np.ascontiguousarray(np.sin(angr).astype(np.float32).T)
    jf = jj.astype(np.float32)
    if_ = ii.astype(np.float32)
    rel = np.stack([np.maximum(if_ - jf, 0.0), np.maximum(jf - if_, 0.0)], 1).astype(np.float32)
    m01 = np.stack([(ii >= jj), (jj >= ii)], 1).astype(np.float32)
    pos = np.arange(128, dtype=np.float32)
    qexp = np.ascontiguousarray(np.broadcast_to(np.stack([pos + 1.0, 128.0 - pos], 0)[None], (128, 2, 128))).astype(np.float32)
    kexp = np.stack([127.0 - pos, pos], 1).astype(np.float32)
    return dict(ident=ident, rotm=rotm, cos_ax=cos_ax, sin_ax=sin_ax, mask_prev=mask_prev, mask_next=mask_next,
                rt_cos=rt_cos, rt_sin=rt_sin, rt_rel=np.ascontiguousarray(rel), rt_m01=np.ascontiguousarray(m01),
                rt_qexp=qexp, rt_kexp=np.ascontiguousarray(kexp))


def fm(v, lead=()):
    v = np.asarray(v, np.float32)
    n = v.shape[-1] // 128
    w = v.reshape(v.shape[:-1] + (n, 128))
    return np.ascontiguousarray(np.moveaxis(w, -1, 0))


def make_in_maps(inputs, ncores=NCORES):
    c = host_consts()
    shared = dict(c)
    shared["modb"] = fm(inputs["mod_b"])
    shared["normg"] = fm(inputs["norm_g"])
    shared["convw"] = np.ascontiguousarray(np.transpose(fm(inputs["ffn_conv_w"]), (0, 1, 3, 2)))
    shared["convb"] = fm(inputs["ffn_conv_b"])
    shared["ga_qkg"] = np.ascontiguousarray(fm(inputs["ga_qk_norm"])[:, :, :, 0])
    shared["ga_sink"] = np.ascontiguousarray(inputs["ga_sink"], np.float32)
    shared["df_qkg"] = np.ascontiguousarray(fm(inputs["df_qk_norm"])[:, 0, :, 0])
    shared["df_subln"] = np.ascontiguousarray(fm(inputs["df_subln"])[:, 0, :])
    shared["df_lam"] = np.ascontiguousarray(inputs["df_lambda"][0], np.float32)
    shared["rt_decay"] = np.ascontiguousarray(inputs["rt_decay"][0].reshape(16), np.float32)
    shared["rt_gn"] = np.ascontiguousarray(fm(inputs["rt_gn"])[:, 0, :, :])
    for k in ("mod_w", "ffn_w_in", "ffn_w_out", "ga_wqkv", "ga_wo", "df_wqkv", "df_wo", "rt_w_in", "rt_wo"):
        shared[k] = np.ascontiguousarray(inputs[k], np.float32)
    maps = []
    for b in range(ncores):
        m = dict(shared)
        m["x"] = np.ascontiguousarray(inputs["x"][b], np.float32)
        m["ctx"] = np.ascontiguousarray(inputs["ctx"][b], np.float32)
        cv = np.stack([fm(inputs["c"][b]), fm(inputs["c_ctx"])], -1)
        m["cvec"] = np.ascontiguousarray(cv)
        maps.append(m)
    return maps


def kernel(**inputs):
    nc = build_program(NL)
    maps = make_in_maps(inputs)
    maps = maps[:NCORES]
    res = run_bass_kernel_spmd(nc, maps, core_ids=list(range(NCORES)))
    if DEBUG_OUT:
        global LAST_RESULTS
        LAST_RESULTS = res.results
    out = np.stack([np.asarray(r["out"], np.float32) for r in res.results], 0)
    return out
```
